# Optimizing a Trainium2 kernel written in Bass

```python
import jax, jax.numpy as jnp
from jax import lax
import numpy as np

D_MODEL = 1024
BATCH = 8
SEQ = 4096
DEPTH = 1

PLE_DIM = 256
ROPE_THETA = 10000.0
RMS_EPS = 1e-6
BLOCK = 128

SWA_WINDOW = 128
A_HEADS = 8
A_KV_HEADS = 2
A_HEAD_DIM = 64
A_WIDTH = A_HEADS * A_HEAD_DIM

B_HEADS = 8
Q_LORA = 256
KV_LORA = 128
NOPE_DIM = 64
ROPE_DIM = 32
V_DIM = 64
B_WIDTH = B_HEADS * V_DIM

SPLIT_SIZES = (A_HEADS * A_HEAD_DIM,
               A_KV_HEADS * A_HEAD_DIM,
               A_KV_HEADS * A_HEAD_DIM,
               Q_LORA,
               KV_LORA,
               ROPE_DIM,
               2 * D_MODEL)
IN_COLS = 768 + 416 + 2 * D_MODEL

D_FF = 2816
CONV_W = 3

kernel_name = "hybrid_swa_mla_gated_block"


def split_cols(t, sizes):
    idx = np.cumsum(np.array(sizes))[:-1].tolist()
    return jnp.split(t, idx, axis=-1)


def rmsnorm(t, g):
    tf = t.astype(jnp.float32)
    y = tf * lax.rsqrt(jnp.mean(tf * tf, axis=-1, keepdims=True) + RMS_EPS)
    return (y * g.astype(jnp.float32)).astype(t.dtype)


def rope_tables(positions, dim):
    inv = ROPE_THETA ** (-(jnp.arange(0, dim, 2, dtype=jnp.float32) / dim))
    ang = positions.astype(jnp.float32)[..., None] * inv
    return jnp.cos(ang), jnp.sin(ang)


def apply_rope(t, cos, sin):
    tf = t.astype(jnp.float32)
    t1, t2 = jnp.split(tf, 2, axis=-1)
    c, s = cos[:, :, None, :], sin[:, :, None, :]
    return jnp.concatenate([t1 * c - t2 * s, t2 * c + t1 * s], axis=-1).astype(t.dtype)


def swa_attention(q, k, v, sinks):
    B_, S_, H, d = q.shape
    G = H // A_KV_HEADS
    nblk = S_ // BLOCK
    qb = q.reshape(B_, nblk, BLOCK, A_KV_HEADS, G, d)
    pad = ((0, 0), (BLOCK, 0), (0, 0), (0, 0))
    kp = jnp.pad(k, pad).reshape(B_, nblk + 1, BLOCK, A_KV_HEADS, d)
    vp = jnp.pad(v, pad).reshape(B_, nblk + 1, BLOCK, A_KV_HEADS, d)
    kb = jnp.concatenate([kp[:, :-1], kp[:, 1:]], axis=2)
    vb = jnp.concatenate([vp[:, :-1], vp[:, 1:]], axis=2)
    s = jnp.einsum('bnqkgd,bnskd->bnkgqs', qb, kb).astype(jnp.float32) * (d ** -0.5)
    qi = jnp.arange(BLOCK)[:, None]
    kj = jnp.arange(2 * BLOCK)[None, :]
    rel = qi + BLOCK - kj
    band = (rel >= 0) & (rel < SWA_WINDOW)
    key_abs = jnp.arange(nblk)[:, None, None] * BLOCK + kj[None] - BLOCK
    valid = band[None] & (key_abs >= 0)
    s = jnp.where(valid[None, :, None, None], s, -jnp.inf)
    sink = sinks.astype(jnp.float32).reshape(1, 1, A_KV_HEADS, G, 1, 1)
    m = jnp.maximum(jnp.max(s, axis=-1, keepdims=True), sink)
    e = jnp.exp(s - m)
    pr = e / (jnp.sum(e, axis=-1, keepdims=True) + jnp.exp(sink - m))
    out = jnp.einsum('bnkgqs,bnskd->bnqkgd', pr.astype(v.dtype), vb)
    return out.reshape(B_, S_, H * d)


def mla_attention(q, k, v):
    B_, S_, H, dqk = q.shape
    dv = v.shape[-1]
    nblk = S_ // BLOCK
    scale = dqk ** -0.5
    qb = q.reshape(B_, nblk, BLOCK, H, dqk).transpose(1, 0, 2, 3, 4)
    key_pos = jnp.arange(S_)

    def one_block(args):
        qblk, n = args
        s = jnp.einsum('bqhd,bshd->bhqs', qblk, k).astype(jnp.float32) * scale
        q_pos = n * BLOCK + jnp.arange(BLOCK)
        causal = key_pos[None, :] <= q_pos[:, None]
        pr = jax.nn.softmax(jnp.where(causal, s, -jnp.inf), axis=-1)
        return jnp.einsum('bhqs,bshd->bqhd', pr.astype(v.dtype), v)

    out = lax.map(one_block, (qb, jnp.arange(nblk)))
    return out.transpose(1, 0, 2, 3, 4).reshape(B_, S_, H * dv)


def causal_dwconv(u, w, b):
    C = u.shape[-1]
    y = lax.conv_general_dilated(u, w[:, None, :].astype(u.dtype), window_strides=(1,),
                                 padding=[(CONV_W - 1, 0)],
                                 dimension_numbers=('NWC', 'WIO', 'NWC'),
                                 feature_group_count=C)
    return y + b


def setup_inputs(seed: int = 0) -> dict:
    key = jax.random.key(seed)
    ks = jax.random.split(key, 24)
    f32 = jnp.float32

    def w(k, shape, fan_in):
        return jax.random.normal(k, shape, f32) * (fan_in ** -0.5)

    def gain(k, dim):
        return 1.0 + 0.02 * jax.random.normal(k, (DEPTH, dim), f32)

    x = jax.random.normal(ks[0], (BATCH, SEQ, D_MODEL), f32)
    p = jax.random.normal(ks[1], (DEPTH, BATCH, SEQ, PLE_DIM), f32)
    start = jax.random.randint(ks[2], (BATCH, 1), 0, 1024, dtype=jnp.int32)
    positions = (start + jnp.arange(SEQ, dtype=jnp.int32)[None, :]).astype(jnp.int32)
    return {
        "x": x,
        "p": p,
        "positions": positions,
        "attn_pre_norm": gain(ks[3], D_MODEL),
        "attn_post_norm": gain(ks[4], D_MODEL),
        "w_in": w(ks[5], (DEPTH, D_MODEL, IN_COLS), D_MODEL),
        "b_gate": 0.01 * jax.random.normal(ks[6], (DEPTH, 2 * D_MODEL), f32),
        "sinks": 0.5 * jax.random.normal(ks[7], (DEPTH, A_HEADS), f32),
        "q_a_norm": gain(ks[8], Q_LORA),
        "w_uq": w(ks[9], (DEPTH, Q_LORA, B_HEADS * (NOPE_DIM + ROPE_DIM)), Q_LORA),
        "kv_a_norm": gain(ks[10], KV_LORA),
        "w_ukv": w(ks[11], (DEPTH, KV_LORA, B_HEADS * (NOPE_DIM + V_DIM)), KV_LORA),
        "w_branch_a": w(ks[12], (DEPTH, A_WIDTH, D_MODEL), A_WIDTH),
        "w_branch_b": w(ks[13], (DEPTH, B_WIDTH, D_MODEL), B_WIDTH),
        "w_out": w(ks[14], (DEPTH, D_MODEL, D_MODEL), D_MODEL),
        "mlp_pre_norm": gain(ks[15], D_MODEL),
        "mlp_post_norm": gain(ks[16], D_MODEL),
        "w_up": w(ks[17], (DEPTH, D_MODEL, 2 * D_FF), D_MODEL),
        "conv_w": w(ks[18], (DEPTH, CONV_W, 2 * D_FF), CONV_W),
        "conv_b": 0.01 * jax.random.normal(ks[19], (DEPTH, 2 * D_FF), f32),
        "w_down": w(ks[20], (DEPTH, D_FF, D_MODEL), D_FF),
        "ple_norm": gain(ks[21], D_MODEL),
        "w_ple_gate": w(ks[22], (DEPTH, D_MODEL, D_MODEL), D_MODEL),
        "w_ple": w(ks[23], (DEPTH, PLE_DIM, D_MODEL), PLE_DIM),
    }


def reference(x, p, positions, attn_pre_norm, attn_post_norm, w_in, b_gate, sinks,
              q_a_norm, w_uq, kv_a_norm, w_ukv, w_branch_a, w_branch_b, w_out,
              mlp_pre_norm, mlp_post_norm, w_up, conv_w, conv_b, w_down,
              ple_norm, w_ple_gate, w_ple):
    B_, S_, _ = x.shape
    cos_a, sin_a = rope_tables(positions, A_HEAD_DIM)
    cos_b, sin_b = rope_tables(positions, ROPE_DIM)
    for i in range(DEPTH):
        h = rmsnorm(x, attn_pre_norm[i])
        qa, ka, va, cq, ckv, kr, gates = split_cols(h @ w_in[i], SPLIT_SIZES)

        qa = apply_rope(qa.reshape(B_, S_, A_HEADS, A_HEAD_DIM), cos_a, sin_a)
        ka = apply_rope(ka.reshape(B_, S_, A_KV_HEADS, A_HEAD_DIM), cos_a, sin_a)
        va = va.reshape(B_, S_, A_KV_HEADS, A_HEAD_DIM)
        ya = swa_attention(qa, ka, va, sinks[i])

        qb = (rmsnorm(cq, q_a_norm[i]) @ w_uq[i]).reshape(B_, S_, B_HEADS, NOPE_DIM + ROPE_DIM)
        q_nope, q_pe = split_cols(qb, (NOPE_DIM, ROPE_DIM))
        q_pe = apply_rope(q_pe, cos_b, sin_b)
        kvb = (rmsnorm(ckv, kv_a_norm[i]) @ w_ukv[i]).reshape(B_, S_, B_HEADS, NOPE_DIM + V_DIM)
        k_nope, vb = split_cols(kvb, (NOPE_DIM, V_DIM))
        k_pe = apply_rope(kr[:, :, None, :], cos_b, sin_b)
        qb = jnp.concatenate([q_nope, q_pe], axis=-1)
        kb = jnp.concatenate([k_nope, jnp.broadcast_to(k_pe, (B_, S_, B_HEADS, ROPE_DIM))], axis=-1)
        yb = mla_attention(qb, kb, vb)

        gate_a, gate_b = jnp.split(jax.nn.sigmoid(gates + b_gate[i]), 2, axis=-1)
        mixed = gate_a * (ya @ w_branch_a[i]) + gate_b * (yb @ w_branch_b[i])
        x = x + rmsnorm(mixed @ w_out[i], attn_post_norm[i])

        h = rmsnorm(x, mlp_pre_norm[i])
        u = causal_dwconv(h @ w_up[i], conv_w[i], conv_b[i])
        u_gate, u_val = jnp.split(u, 2, axis=-1)
        ff = (jax.nn.gelu(u_gate, approximate=True) * u_val) @ w_down[i]
        x = x + rmsnorm(ff, mlp_post_norm[i])

        e = p[i] @ w_ple[i]
        x = x + jax.nn.sigmoid(rmsnorm(x, ple_norm[i]) @ w_ple_gate[i]) * e
    return x
```

```python
import numpy as np
from concourse.bass_utils import run_bass_kernel_spmd
from contextlib import ExitStack
import concourse.bass as bass
import concourse.mybir as mybir

F32 = mybir.dt.float32
BF16 = mybir.dt.bfloat16
I32 = mybir.dt.int32
AF = mybir.ActivationFunctionType
ALU = mybir.AluOpType
AX = mybir.AxisListType


class Buf:
    __slots__ = ("name", "last_w", "readers", "sem", "ndma")

    def __init__(self, name):
        self.name = name
        self.last_w = None
        self.readers = []
        self.sem = None
        self.ndma = 0


class Ins:
    __slots__ = ("eng", "fn", "deps", "dma", "home", "dval", "signal", "sigidx", "snap", "waits", "stage")

    def __init__(self, eng, fn, dma=False, home=None):
        self.eng = eng
        self.fn = fn
        self.deps = []
        self.dma = dma
        self.home = home
        self.dval = 0
        self.signal = False
        self.sigidx = 0
        self.snap = None
        self.waits = None


class Sched:
    ENGS = ("pe", "act", "dve", "pool", "sp")

    def __init__(self, nc, stack):
        self.nc = nc
        self.stack = stack
        self.prog = []
        self.esem = {e: stack.enter_context(nc.semaphore("s_" + e)) for e in self.ENGS}
        self.dma_bufs = []
        self.nbuf = 0
        self.cur = "init"

    def sbuf(self, name, shape, dt):
        return self.stack.enter_context(self.nc.sbuf_tensor(name, list(shape), dt))

    def psum(self, name, shape, dt):
        return self.stack.enter_context(self.nc.psum_tensor(name, list(shape), dt))

    def buf(self, name=None):
        self.nbuf += 1
        return Buf(name or ("b%d" % self.nbuf))

    def _add(self, ins, reads, writes):
        deps = []
        for b in reads:
            if b.last_w is not None:
                deps.append(b.last_w)
        for b in writes:
            if b.last_w is not None:
                deps.append(b.last_w)
            deps.extend(b.readers)
        for d in deps:
            if d is ins:
                continue
            if not d.dma and not ins.dma and d.eng == ins.eng:
                if ins.eng == "pe":
                    continue
            ins.deps.append(d)
        for b in reads:
            b.readers.append(ins)
        for b in writes:
            b.last_w = ins
            b.readers = []
        ins.stage = self.cur
        self.prog.append(ins)
        return ins

    def op(self, eng, fn, reads=(), writes=()):
        ins = Ins(eng, fn)
        deps_raw = set()
        for b in reads:
            if b.last_w is not None:
                deps_raw.add(id(b.last_w))
        self._add(ins, reads, writes)
        if eng != "pool":
            ins.deps = [d for d in ins.deps if d.dma or d.eng != eng or id(d) in deps_raw]
        for d in ins.deps:
            if not d.dma:
                d.signal = True
        return ins

    def dma(self, eng, fn, home, reads=(), writes=()):
        if home.sem is None:
            home.sem = self.stack.enter_context(self.nc.semaphore("d_" + home.name))
            self.dma_bufs.append(home)
        ins = Ins(eng, fn, dma=True, home=home)
        self._add(ins, reads, writes)
        home.ndma += 1
        ins.dval = 16 * home.ndma
        for d in ins.deps:
            if not d.dma:
                d.signal = True
        return ins

    def emit(self):
        nc = self.nc
        known = {e: {} for e in self.ENGS}
        sigcnt = {e: 0 for e in self.ENGS}
        streams = {e: [] for e in self.ENGS}
        issued = {}
        nwaits = 0
        for ins in self.prog:
            e = ins.eng
            kn = known[e]
            need = {}
            for d in ins.deps:
                if d.dma:
                    sem = d.home.sem
                    key = ("d", id(d.home))
                    val = 16 * issued[id(d.home)]
                else:
                    sem = self.esem[d.eng]
                    key = ("e", d.eng)
                    val = d.sigidx
                    assert val > 0
                if kn.get(key, 0) >= val:
                    continue
                if key not in need or need[key][1] < val:
                    need[key] = (sem, val, d)
            for key, (sem, val, d) in need.items():
                if kn.get(key, 0) >= val:
                    continue
                streams[e].append(("w", sem, val))
                nwaits += 1
                kn[key] = val
                if not d.dma and d.snap is not None:
                    for k2, v2 in d.snap.items():
                        if kn.get(k2, 0) < v2:
                            kn[k2] = v2
            if ins.dma:
                issued[id(ins.home)] = issued.get(id(ins.home), 0) + 1
            else:
                if ins.signal:
                    sigcnt[e] += 1
                    ins.sigidx = sigcnt[e]
                    ins.snap = dict(kn)
                    ins.snap[("e", e)] = ins.sigidx
            streams[e].append(("i", ins))
        for b in self.dma_bufs:
            streams["sp"].append(("w", b.sem, 16 * b.ndma))
        self.stats = dict(n_ins=len(self.prog), n_waits=nwaits,
                          per_eng={e: len(streams[e]) for e in self.ENGS},
                          nsem=len(self.dma_bufs) + 5)
        esem = self.esem
        self.labels = {e: [it[1].stage for it in streams[e] if it[0] == "i" and not it[1].dma] for e in self.ENGS}

        def run(engobj, lst, e):
            for item in lst:
                if item[0] == "w":
                    engobj.wait_ge(item[1], item[2])
                else:
                    ins = item[1]
                    bi = ins.fn(engobj)
                    if ins.dma:
                        bi.then_inc(ins.home.sem, 16)
                    elif ins.signal:
                        bi.then_inc(esem[e], 1)

        with nc.Block() as block:
            @block.sync
            def _(eng):
                run(eng, streams["sp"], "sp")

            @block.tensor
            def _(eng):
                run(eng, streams["pe"], "pe")

            @block.scalar
            def _(eng):
                run(eng, streams["act"], "act")

            @block.vector
            def _(eng):
                run(eng, streams["dve"], "dve")

            @block.gpsimd
            def _(eng):
                run(eng, streams["pool"], "pool")


D = 1024
TT = 512
EPS = 1e-6
SC_A = 64 ** -0.5
SC_B = 96 ** -0.5
C1_2PI = 6.28125
C2_2PI = 2 * np.pi - 6.28125


def build_program(SEQ, dbg=False):
    NT = SEQ // TT
    NBLK = SEQ // 128
    nc = bass.Bass("TRN2", target_bir_lowering=False)

    def din(name, shape, dt=F32):
        return nc.dram_tensor(name, list(shape), dt, kind="ExternalInput").ap()

    x_d = din("x", [SEQ, D])
    p_d = din("p", [SEQ, 256])
    pos_d = din("pos", [1, SEQ], I32)
    cst_d = din("cst", [128, 4])
    w_in_d = din("w_in", [D, 3232])
    w_uq_d = din("w_uq", [256, 768])
    w_ukv_d = din("w_ukv", [128, 1024])
    w_a_d = din("w_branch_a", [512, D])
    w_b_d = din("w_branch_b", [512, D])
    w_out_d = din("w_out", [D, D])
    w_up_d = din("w_up", [D, 5632])
    w_dn_d = din("w_down", [2816, D])
    w_pg_d = din("w_ple_gate", [D, D])
    w_ple_d = din("w_ple", [256, D])
    conv_w_d = din("conv_w", [3, 5632])
    conv_b_d = din("conv_b", [1, 5632])
    b_gate_d = din("b_gate", [1, 2048])
    sinks_d = din("sinks", [1, 8])
    g_attn_pre_d = din("attn_pre_norm", [1, D])
    g_attn_post_d = din("attn_post_norm", [1, D])
    g_mlp_pre_d = din("mlp_pre_norm", [1, D])
    g_mlp_post_d = din("mlp_post_norm", [1, D])
    g_ple_d = din("ple_norm", [1, D])
    g_q_d = din("q_a_norm", [1, 256])
    g_kv_d = din("kv_a_norm", [1, 128])
    out_d = nc.dram_tensor("out", [SEQ, D], F32, kind="ExternalOutput").ap()

    def scr(name, shape):
        return nc.dram_tensor(name, list(shape), BF16, kind="Internal").ap()

    WIN_s = scr("WIN_s", [8, 128, 2, 8, 128])
    WUQ_s = scr("WUQ_s", [2, 128, 2, 768])
    WUKV_s = scr("WUKV_s", [128, 1, 1024])
    WG_s = scr("WG_s", [8, 128, 8, 256])
    WAB_s = scr("WAB_s", [8, 128, 8, 128])
    WOUT_s = scr("WOUT_s", [2, 128, 8, 512])
    WUP_s = scr("WUP_s", [22, 128, 2, 8, 128])
    WDN_s = scr("WDN_s", [4, 128, 11, 512])
    WPG_s = scr("WPG_s", [2, 128, 8, 512])
    WPLE_s = scr("WPLE_s", [128, 2, 1024])
    KC_s = scr("KC_s", [8, 128, SEQ])
    VC_s = scr("VC_s", [8, 128, NBLK, 128])

    with ExitStack() as st:
        S = Sched(nc, st)
        sb = S.sbuf
        B = S.buf

        xt = sb("xt", [128, 4, D], F32); b_xtj = [B("xt%d" % j) for j in range(4)]
        hn = sb("hn", [128, D], BF16); b_hn = B("hn")
        hn2 = [hn, sb("hn_b", [128, D], BF16)]; b_hn2 = [b_hn, B("hn_b")]
        stat = sb("stat", [128, 4, 16], F32); b_stat = [B("stat%d" % j) for j in range(4)]; b_pn = [B("pn%d" % j) for j in range(4)]
        hT = sb("hT", [128, 8, TT], BF16); b_hT = B("hT")
        QA = sb("QA", [128, 4, TT], BF16); b_QA = [B() for _ in range(4)]
        KAb = sb("KAb", [128, 640], BF16); b_KA = B("KA")
        VAb = sb("VAb", [128, 5, 192], BF16); b_VA = B("VA")
        cqb = sb("cqb", [128, 2, TT], BF16); b_cqb = B()
        sqq = sb("sqq", [128, 2, TT], BF16); b_sqq = B()
        ckvb = sb("ckvb", [128, TT], BF16); b_ckvb = B()
        sqkv = sb("sqkv", [128, TT], BF16); b_sqkv = B()
        rstdq = sb("rstdq", [128, TT], F32); b_rq = B()
        rstdkv = sb("rstdkv", [128, TT], F32); b_rkv = B()
        cosA = sb("cosA", [128, TT], F32); sinA = sb("sinA", [128, TT], F32); b_tabA = B()
        cosB = sb("cosB", [128, TT], F32); sinB = sb("sinB", [128, TT], F32); b_tabB = B()
        NTMP = 12
        tmpf = [sb("tmpf%d" % i, [128, 514], F32) for i in range(NTMP)]
        b_tmp = [B("tmp%d" % i) for i in range(NTMP)]
        QB = sb("QB", [128, 2, TT], BF16); b_QB = [B(), B()]
        NPT = 5
        ptb = [sb("pt%d" % i, [128, TT], BF16) for i in range(NPT)]; b_pt = [B() for _ in range(NPT)]
        ya = sb("ya", [128, 4, TT], BF16); b_ya = B("ya")
        yb = sb("yb", [128, 4, TT], BF16); b_yb = B("yb")
        b_mixed = B("mixed")
        kbuf = [sb("kbuf%d" % i, [128, SEQ], BF16) for i in range(2)]; b_kbuf = [B("kbuf0"), B("kbuf1")]
        vbuf = [sb("vbuf%d" % i, [128, NBLK, 128], BF16) for i in range(2)]; b_vbuf = [B("vbuf0"), B("vbuf1")]
        act = sb("act", [128, 22, TT], BF16); b_act = B("act")
        b_kst = B("kst"); b_vones = B("vones"); b_vst_e = [B("vste%d" % j) for j in range(4)]; b_vst_o = [B("vsto%d" % j) for j in range(4)]
        kst = act[:, 0:8, :]
        mixed = act[:, 0:8, :]
        vst = act[:, 8:16, :].rearrange("p a (b c) -> p (a b) c", c=128).rearrange("p (j h) c -> p j h c", h=8)
        gpost = [sb("gpost%d" % i, [128, D], F32) for i in range(2)]; b_gpost = [B("gpost0"), B("gpost1")]
        NRING = 4
        ring = [sb("ring%d" % i, [128, 2048], BF16) for i in range(NRING)]; b_ring = [B("ring%d" % i) for i in range(NRING)]
        big = [sb("big%d" % i, [128, 5632], BF16) for i in range(2)]; b_big = [B("big0"), B("big1")]
        ptT = sb("ptT", [128, 2, TT], BF16); b_ptT = B()
        halo = sb("halo", [128, 44, 2], F32); b_halo = B("halo"); b_halo_i = [B("halo%d" % i) for i in range(44)]
        R1 = sb("R1", [128, 128], F32); R2 = sb("R2", [128, 128], F32); b_R = B()
        C1 = sb("C1", [128, 128], F32); C2 = sb("C2", [128, 128], F32); b_C = B()
        identf = sb("identf", [128, 128], F32); identb = sb("identb", [128, 128], BF16); b_id = B()
        onesf = sb("onesf", [128, 128], F32); onesb = sb("onesb", [128, 128], BF16); b_ones = B()
        maskf = sb("maskf", [128, 256], F32); maskb = sb("maskb", [128, 256], BF16); b_mask = B(); maskbias = sb("maskbias", [128, 256], BF16)
        esink = sb("esink", [128, 8], F32); b_esink = B()
        cst = sb("cst_sb", [128, 4], F32); b_cst = B()
        small = sb("small", [128, 32], F32); b_small = B("small")
        negones = None

        PS = [S.psum("ps%d" % i, [128, 512], F32) for i in range(8)]
        b_ps = [B("ps%d" % i) for i in range(8)]
        PSB = [p[:].bitcast(BF16) for p in PS]
        ring_ctr = [0]

        ring_banks = [(0, 1, 2, 3)]

        def gps():
            rb = ring_banks[0]
            i = rb[ring_ctr[0] % len(rb)]
            ring_ctr[0] += 1
            return PS[i], b_ps[i]
        o_ctr = [0]

        def ops_():
            i = 4 + (o_ctr[0] % 2)
            o_ctr[0] += 1
            return PS[i], b_ps[i]
        t_ctr = [0]

        def tps():
            i = 6 + (t_ctr[0] % 2)
            t_ctr[0] += 1
            return PSB[i], PS[i], b_ps[i]
        tmp_ctr = [0]

        def gtmp():
            i = tmp_ctr[0] % NTMP
            tmp_ctr[0] += 1
            return tmpf[i], b_tmp[i]
        pt_ctr = [0]

        def gpt():
            i = pt_ctr[0] % NPT
            pt_ctr[0] += 1
            return ptb[i], b_pt[i]
        rr_ctr = [0]

        def gring():
            i = rr_ctr[0] % NRING
            rr_ctr[0] += 1
            return ring[i], b_ring[i]
        ew_ctr = [0]

        def ew():
            ew_ctr[0] += 1
            return "dve" if ew_ctr[0] % 2 else "pool"

        DBG = {}

        def dump(name, ap, buf, shape, dt=F32):
            if not dbg:
                return
            d = nc.dram_tensor("dbg_" + name, list(shape), dt, kind="ExternalOutput").ap()
            DBG[name] = d
            S.dma("sp", lambda e: e.dma_start(out=d, in_=ap), buf, reads=[buf])

        S.dma("sp", lambda e: e.dma_start(out=cst[:], in_=cst_d[:, :]), b_cst, writes=[b_cst])
        S.op("pool", lambda e: e.memset(onesf[:], 1.0), writes=[b_ones])
        S.op("pool", lambda e: e.memset(onesb[:], 1.0), writes=[b_ones])
        S.op("pool", lambda e: e.affine_select(out=identf[:], in_=onesf[:], pattern=[[-1, 128]], compare_op=ALU.is_equal, fill=0.0, base=0, channel_multiplier=1), reads=[b_ones], writes=[b_id])
        S.op("pool", lambda e: e.tensor_copy(out=identb[:], in_=identf[:]), reads=[b_id], writes=[b_id])
        S.op("pool", lambda e: e.affine_select(out=maskf[:, 0:128], in_=onesf[:], pattern=[[1, 128]], compare_op=ALU.is_ge, fill=0.0, base=0, channel_multiplier=-1), reads=[b_ones], writes=[b_mask])
        S.op("pool", lambda e: e.affine_select(out=maskf[:, 128:256], in_=onesf[:], pattern=[[-1, 128]], compare_op=ALU.is_ge, fill=0.0, base=-1, channel_multiplier=1), reads=[b_ones], writes=[b_mask])
        S.op("pool", lambda e: e.tensor_copy(out=maskb[:], in_=maskf[:]), reads=[b_mask], writes=[b_mask])
        S.op("pool", lambda e: e.tensor_scalar(out=maskbias[:], in0=maskf[:], scalar1=-1.0, scalar2=30000.0, op0=ALU.add, op1=ALU.mult), reads=[b_mask], writes=[b_mask])
        S.op("pool", lambda e: e.memset(halo[:], 0.0), writes=[b_halo] + b_halo_i)
        S.op("pool", lambda e: e.memset(KAb[:], 0.0), writes=[b_KA])
        S.op("pool", lambda e: e.memset(VAb[:], 0.0), writes=[b_VA])
        S.op("pool", lambda e: e.memset(VAb[:, :, 64:128], 1.0), writes=[b_VA])
        S.op("pool", lambda e: e.memset(R1[:], 0.0), writes=[b_R])
        S.op("pool", lambda e: e.memset(R2[:], 0.0), writes=[b_R])
        def rload(Rt, r0, n, src):
            S.dma("sp", lambda e: e.dma_start(out=Rt[r0:r0 + n, :], in_=src.rearrange("(i p) -> i p", p=128)), b_R, writes=[b_R])
        rload(R1, 0, 44, conv_w_d[0, :]); rload(R1, 44, 44, conv_w_d[1, :])
        rload(R1, 88, 2, g_q_d[0, :]); rload(R1, 90, 1, g_kv_d[0, :])
        rload(R2, 0, 44, conv_w_d[2, :]); rload(R2, 44, 44, conv_b_d[0, :]); rload(R2, 88, 16, b_gate_d[0, :])
        rload(R2, 104, 8, g_attn_pre_d[0, :]); rload(R2, 112, 8, g_mlp_pre_d[0, :]); rload(R2, 120, 8, g_ple_d[0, :])
        for Rt, Ct, bank in ((R1, C1, 0), (R2, C2, 1)):
            S.op("pe", lambda e, Rt=Rt, bank=bank: e.transpose(out=PS[bank][:, 0:128], in_=Rt[:], identity=identf[:]), reads=[b_R, b_id], writes=[b_ps[bank]])
            S.op("dve", lambda e, Ct=Ct, bank=bank: e.tensor_copy(out=Ct[:], in_=PS[bank][:, 0:128]), reads=[b_ps[bank]], writes=[b_C])
        CW = [C1[:, 0:44], C1[:, 44:88], C2[:, 0:44]]
        CBIAS = C2[:, 44:88]
        BG = C2[:, 88:104]
        G_Q = C1[:, 88:90]
        G_KV = C1[:, 90:91]
        G_PRE = {"attn": C2[:, 104:112], "mlp": C2[:, 112:120], "ple": C2[:, 120:128]}
        S.dma("sp", lambda e: e.dma_start(out=gpost[0][:], in_=g_attn_post_d[0:1, :].partition_broadcast(128)), b_gpost[0], writes=[b_gpost[0]])
        S.dma("sp", lambda e: e.dma_start(out=gpost[1][:], in_=g_mlp_post_d[0:1, :].partition_broadcast(128)), b_gpost[1], writes=[b_gpost[1]])
        S.dma("sp", lambda e: e.dma_start(out=small[:, 0:8], in_=sinks_d[0:1, :].partition_broadcast(128)), b_small, writes=[b_small])
        S.op("act", lambda e: e.activation(out=esink[:], in_=small[:, 0:8], func=AF.Exp), reads=[b_small], writes=[b_esink])

        dump("C1", C1[:], b_C, [128, 128]); dump("C2", C2[:], b_C, [128, 128]); dump("esink", esink[:], b_esink, [128, 8])
        dump("maskb", maskb[:], b_mask, [128, 256], BF16)
        S.cur = "prepass"
        b_scr = {}

        def sbuf_of(name):
            if name not in b_scr:
                b_scr[name] = B("scr_" + name)
            return b_scr[name]
        pp_ctr = [0]
        stg_f = [xt[:, 0:2, :].rearrange("p a b -> p (a b)"), xt[:, 2:4, :].rearrange("p a b -> p (a b)")]
        b_stg_f = [B("stgf0"), B("stgf1")]
        if SEQ >= 4096:
            stg_f += [kbuf[0][:, :].bitcast(F32), kbuf[1][:, :].bitcast(F32), vbuf[0][:].rearrange("p a b -> p (a b)").bitcast(F32)]
            b_stg_f += [b_kbuf[0], b_kbuf[1], b_vbuf[0]]
        NSTG = len(stg_f)
        act_flat = act[:].rearrange("p a b -> p (a b)")
        stg_b = [act_flat[:, i * 2048:(i + 1) * 2048] for i in range(NSTG)]
        b_stg_b = [B("stgb%d" % i) for i in range(NSTG)]
        cast_ctr = [0]

        def cast_eng():
            cast_ctr[0] += 1
            return ("dve", "act")[cast_ctr[0] % 2]

        def piece(dst, scr_name, KC, N, loads, g=None, negs=(), zero=False, cast_fn=None, rot=None):
            i = pp_ctr[0] % NSTG
            pp_ctr[0] += 1
            sf = stg_f[i][:, 0:KC * N].rearrange("p (k n) -> p k n", n=N)
            sbv = stg_b[i][:, 0:KC * N].rearrange("p (k n) -> p k n", n=N)
            bf_, bb_ = b_stg_f[i], b_stg_b[i]
            if zero:
                S.op("pool", lambda e: e.memset(sf, 0.0), writes=[bf_])
            for ld in loads:
                if len(ld) == 3:
                    c0, n, src = ld
                    S.dma("sp", lambda e, c0=c0, n=n, src=src: e.dma_start(out=sf[:, :, c0:c0 + n], in_=src), bf_, writes=[bf_])
                elif len(ld) == 2:
                    dfn, src = ld
                    S.dma("sp", lambda e, dfn=dfn, src=src: e.dma_start(out=dfn(sf), in_=src), bf_, writes=[bf_])
                else:
                    c0, n, src, p0, p1, k0 = ld
                    S.dma("sp", lambda e, c0=c0, n=n, src=src, p0=p0, p1=p1, k0=k0: e.dma_start(out=sf[p0:p1, k0, c0:c0 + n], in_=src), bf_, writes=[bf_])
            if cast_fn is not None:
                cast_fn(sf, sbv, [bf_], [bb_])
            elif g is None:
                en = cast_eng()
                if en == "act":
                    S.op("act", lambda e: e.copy(out=sbv, in_=sf), reads=[bf_], writes=[bb_])
                else:
                    S.op(en, lambda e: e.tensor_copy(out=sbv, in_=sf), reads=[bf_], writes=[bb_])
            else:
                for kc in range(KC):
                    en = cast_eng()
                    if en == "act":
                        S.op("act", lambda e, kc=kc: e.activation(out=sbv[:, kc, :], in_=sf[:, kc, :], func=AF.Identity, scale=g[:, kc:kc + 1]), reads=[bf_, b_C], writes=[bb_])
                    else:
                        S.op(en, lambda e, kc=kc: e.tensor_scalar(out=sbv[:, kc, :], in0=sf[:, kc, :], scalar1=g[:, kc:kc + 1], scalar2=None, op0=ALU.mult), reads=[bf_, b_C], writes=[bb_])
            for (c0, n) in negs:
                S.op("dve", lambda e, c0=c0, n=n: e.tensor_scalar(out=sbv[:, :, c0:c0 + n], in0=sbv[:, :, c0:c0 + n], scalar1=-1.0, scalar2=None, op0=ALU.mult), reads=[bb_], writes=[bb_])
            bs = sbuf_of(scr_name)
            S.dma("pool", lambda e: e.dma_start(out=dst, in_=sbv), bb_, reads=[bb_], writes=[bs])
            if rot is not None:
                dst_rot, blocks, zero_rot = rot
                i2 = pp_ctr[0] % NSTG
                pp_ctr[0] += 1
                sb2 = stg_b[i2][:, 0:KC * N].rearrange("p (k n) -> p k n", n=N)
                bb2 = b_stg_b[i2]
                if zero_rot:
                    S.op("pool", lambda e: e.memset(sb2, 0.0), writes=[bb2])
                for (b0, hd) in blocks:
                    hh = hd // 2
                    S.op("dve", lambda e, b0=b0, hh=hh: e.tensor_scalar(out=sb2[:, :, b0:b0 + hh], in0=sbv[:, :, b0 + hh:b0 + 2 * hh], scalar1=-1.0, scalar2=None, op0=ALU.mult), reads=[bb_], writes=[bb2])
                    S.op("act", lambda e, b0=b0, hh=hh: e.copy(out=sb2[:, :, b0 + hh:b0 + 2 * hh], in_=sbv[:, :, b0:b0 + hh]), reads=[bb_], writes=[bb2])
                S.dma("pool", lambda e: e.dma_start(out=dst_rot, in_=sb2), bb2, reads=[bb2], writes=[bs])

        def wk(w, c0, n):
            return w[:, c0:c0 + n].rearrange("(k p) n -> p k n", p=128)

        gpre = G_PRE["attn"]
        def win_dst(u, sl):
            return WIN_s[u, :, sl, :, :]
        for c in range(4):
            ba, bb2 = c * 64, (c + 4) * 64
            piece(win_dst(c, 0), "WIN%d" % c, 8, 128, [(0, 64, wk(w_in_d, ba, 64)), (64, 64, wk(w_in_d, bb2, 64))], g=gpre,
                  rot=(win_dst(c, 1), [(0, 64), (64, 64)], False))
        piece(win_dst(4, 0), "WIN4", 8, 128, [(0, 128, wk(w_in_d, 512, 128))], g=gpre, rot=(win_dst(4, 1), [(0, 64), (64, 64)], False))
        piece(win_dst(5, 0), "WIN5", 8, 128, [(0, 128, wk(w_in_d, 768, 128))], g=gpre)
        piece(win_dst(5, 1), "WIN5", 8, 128, [(0, 128, wk(w_in_d, 896, 128))], g=gpre)
        piece(win_dst(6, 0), "WIN6", 8, 128, [(0, 128, wk(w_in_d, 1024, 128))], g=gpre)
        piece(win_dst(6, 1), "WIN6", 8, 128, [(0, 128, wk(w_in_d, 640, 128))], g=gpre)
        piece(win_dst(7, 0), "WIN7", 8, 128, [(64, 32, wk(w_in_d, 1152, 32))], g=gpre, zero=True, rot=(win_dst(7, 1), [(64, 32)], True))
        def uq_rot_cast(sf, sbv, rd, wr):
            sf4 = sf.rearrange("p k (h c) -> p k h c", c=96)
            sb4 = sbv.rearrange("p k (h c) -> p k h c", c=96)
            S.op("pool", lambda e: e.memset(sbv, 0.0), writes=wr)
            S.op("dve", lambda e: e.tensor_scalar(out=sb4[:, :, :, 64:80], in0=sf4[:, :, :, 80:96], scalar1=-1.0, scalar2=None, op0=ALU.mult), reads=rd, writes=wr)
            S.op("dve", lambda e: e.tensor_copy(out=sb4[:, :, :, 80:96], in_=sf4[:, :, :, 64:80]), reads=rd, writes=wr)
        piece(WUQ_s[0], "WUQ", 2, 768, [(0, 768, wk(w_uq_d, 0, 768))])
        piece(WUQ_s[1], "WUQ", 2, 768, [(0, 768, wk(w_uq_d, 0, 768))], cast_fn=uq_rot_cast)

        def ukv_cast(sf, sbv, rd, wr):
            src = sf[:, 0, :].rearrange("p (h two d) -> p two h d", two=2, d=64)
            dstv = sbv[:, 0, :].rearrange("p (two h d) -> p two h d", two=2, d=64)
            S.op("dve", lambda e: e.tensor_copy(out=dstv[:, 0], in_=src[:, 0]), reads=rd, writes=wr)
            S.op("act", lambda e: e.copy(out=dstv[:, 1], in_=src[:, 1]), reads=rd, writes=wr)
        piece(WUKV_s, "WUKV", 1, 1024, [(0, 1024, wk(w_ukv_d, 0, 1024))], cast_fn=ukv_cast)
        for n in range(8):
            piece(WG_s[n], "WG%d" % n, 8, 256, [(0, 128, wk(w_in_d, 1184 + 128 * n, 128)), (128, 128, wk(w_in_d, 2208 + 128 * n, 128))], g=gpre)
        for n in range(8):
            cs = slice(n * 128, (n + 1) * 128)
            lds = [(lambda sf: sf[0:64, 0:4, :], w_a_d[0:256, cs].rearrange("(c p) n -> p c n", p=64)),
                   (lambda sf: sf[64:128, 0:4, :], w_a_d[256:512, cs].rearrange("(c p) n -> p c n", p=64)),
                   (lambda sf: sf[:, 4:8, :], w_b_d[:, cs].rearrange("(k p) n -> p k n", p=128))]
            piece(WAB_s[n], "WAB%d" % n, 8, 128, lds)
        for nh in range(2):
            for kh in range(2):
                piece(WOUT_s[nh, :, kh * 4:(kh + 1) * 4, :], "WOUT%d" % nh, 4, 512, [(0, 512, w_out_d[kh * 512:(kh + 1) * 512, nh * 512:(nh + 1) * 512].rearrange("(k p) n -> p k n", p=128))])
        gm = G_PRE["mlp"]
        for f in range(22):
            for gv in range(2):
                piece(WUP_s[f, :, gv, :, :], "WUP%d" % f, 8, 128, [(0, 128, wk(w_up_d, gv * 2816 + f * 128, 128))], g=gm)
        for nh in range(2):
            for fh in range(2):
                u = nh * 2 + fh
                for (f0, nf) in ((0, 4), (4, 4), (8, 3)):
                    r0 = (fh * 11 + f0) * 128
                    piece(WDN_s[u, :, f0:f0 + nf, :], "WDN%d" % u, nf, 512, [(0, 512, w_dn_d[r0:r0 + nf * 128, nh * 512:(nh + 1) * 512].rearrange("(k p) n -> p k n", p=128))])
        gp = G_PRE["ple"]
        for nh in range(2):
            for kh in range(2):
                piece(WPG_s[nh, :, kh * 4:(kh + 1) * 4, :], "WPG%d" % nh, 4, 512, [(0, 512, w_pg_d[kh * 512:(kh + 1) * 512, nh * 512:(nh + 1) * 512].rearrange("(k p) n -> p k n", p=128))], g=gp[:, kh * 4:(kh + 1) * 4])
        piece(WPLE_s, "WPLE", 2, 1024, [(0, 1024, wk(w_ple_d, 0, 1024))])

        def load_ring(src, scr_name, nelem):
            rt, brt = gring()
            v = rt[:, 0:nelem]
            S.dma("sp", lambda e: e.dma_start(out=v, in_=src), brt, reads=[sbuf_of(scr_name)], writes=[brt])
            return rt, brt

        def load_big(i, src, scr_name, nelem):
            v = big[i][:, 0:nelem]
            S.dma("sp", lambda e: e.dma_start(out=v, in_=src), b_big[i], reads=[sbuf_of(scr_name)], writes=[b_big[i]])

        def mm(ps, lhsT, rhs, start, stop, reads, bps):
            S.op("pe", lambda e: e.matmul(ps, lhsT=lhsT, rhs=rhs, start=start, stop=stop, skip_group_check=True), reads=reads, writes=[bps])

        def norm_T(tag):
            for j in range(4):
                ssq = stat[:, j, 0:1]; rt_ = stat[:, j, 1:2]; rs = stat[:, j, 2:3]
                bs_ = b_stat[j]
                jt, bjt = gtmp()
                junk = jt[:, 0:512].bitcast(BF16)
                S.op("pool", lambda e, ssq=ssq: e.memset(ssq, 0.0), writes=[bs_])
                S.op("act", lambda e, j=j, junk=junk, ssq=ssq: e.activation(out=junk, in_=xt[:, j, :], func=AF.Square, accum_out=ssq), reads=[b_xtj[j]], writes=[bjt, bs_])
                S.op("act", lambda e, rt_=rt_, ssq=ssq: e.activation(out=rt_, in_=ssq, func=AF.Ln, scale=1.0 / D, bias=EPS), reads=[bs_], writes=[bs_])
                S.op("act", lambda e, rs=rs, rt_=rt_: e.activation(out=rs, in_=rt_, func=AF.Exp, scale=-0.5), reads=[bs_], writes=[bs_])
            for j in range(4):
                rs = stat[:, j, 2:3]
                bs_ = b_stat[j]
                hnb = hn2[j % 2]; bhnb = b_hn2[j % 2]
                S.op("dve", lambda e, j=j, hnb=hnb, rs=rs: e.tensor_scalar(out=hnb[:], in0=xt[:, j, :], scalar1=rs, scalar2=None, op0=ALU.mult), reads=[b_xtj[j], bs_], writes=[bhnb])
                pb, pf, bp = tps()
                for c in range(8):
                    S.op("pe", lambda e, c=c, pb=pb, hnb=hnb: e.transpose(out=pb[:, c * 128:(c + 1) * 128], in_=hnb[:, c * 128:(c + 1) * 128], identity=identb[:]), reads=[bhnb, b_id], writes=[bp])
                S.op("act", lambda e, j=j, pb=pb: e.copy(out=hT[:, :, j * 128:(j + 1) * 128], in_=pb[:, 0:1024].rearrange("p (c t) -> p c t", t=128)), reads=[bp], writes=[b_hT])

        def proj_fm(wv, bw, rhsT, brhs, KC=8):
            ps, bps = gps()
            for kc in range(KC):
                mm(ps[:, :], wv[:, kc, :], rhsT[:, kc, :], kc == 0, kc == KC - 1, [bw, brhs], bps)
            return ps, bps

        def trig(inv_col, cos_t, sin_t, btab, p0, p1, posf_, b_posf, posi, b_posi):
            a_t, ba = gtmp(); k_t, bk = gtmp()
            ang = a_t[p0:p1, 0:TT]
            kf = k_t[p0:p1, 0:TT]
            ki = posi[p0:p1, :]
            for (shift, outt) in ((0.0, sin_t), (np.pi / 2, cos_t)):
                S.op("dve", lambda e, shift=shift: e.tensor_scalar(out=ang, in0=posf_[p0:p1, :], scalar1=cst[p0:p1, inv_col:inv_col + 1], scalar2=shift, op0=ALU.mult, op1=ALU.add), reads=[b_posf, b_cst], writes=[ba])
                S.op("dve", lambda e: e.tensor_scalar(out=kf, in0=ang, scalar1=float(1.0 / (2 * np.pi)), scalar2=None, op0=ALU.mult), reads=[ba], writes=[bk])
                S.op("dve", lambda e: e.tensor_copy(out=ki, in_=kf), reads=[bk], writes=[b_posi])
                S.op("dve", lambda e: e.tensor_copy(out=kf, in_=ki), reads=[b_posi], writes=[bk])
                S.op("dve", lambda e: e.scalar_tensor_tensor(out=ang, in0=kf, scalar=-C1_2PI, in1=ang, op0=ALU.mult, op1=ALU.add), reads=[bk, ba], writes=[ba])
                S.op("dve", lambda e: e.scalar_tensor_tensor(out=ang, in0=kf, scalar=-C2_2PI, in1=ang, op0=ALU.mult, op1=ALU.add), reads=[bk, ba], writes=[ba])
                S.op("dve", lambda e: e.tensor_scalar(out=ang, in0=ang, scalar1=-3.1415925, scalar2=3.1415925, op0=ALU.max, op1=ALU.min), reads=[ba], writes=[ba])
                S.op("act", lambda e, outt=outt: e.activation(out=outt[p0:p1, :], in_=ang, func=AF.Sin), reads=[ba], writes=[btab])

        def post_norm_residual_multi(items, gi):
            junks = []
            for (j, p0, bp0, p1, bp1) in items:
                ssq = stat[:, j, 8:10]
                bq = b_pn[j]
                jt, bjt = gtmp()
                junk = jt[:, 0:256].bitcast(BF16)
                S.op("pool", lambda e, ssq=ssq: e.memset(ssq, 0.0), writes=[bq])
                S.op("act", lambda e, junk=junk, p0=p0, ssq=ssq: e.activation(out=junk, in_=p0[:, :], func=AF.Square, accum_out=ssq[:, 0:1]), reads=[bp0], writes=[bjt, bq])
                S.op("act", lambda e, junk=junk, p1=p1, ssq=ssq: e.activation(out=junk, in_=p1[:, :], func=AF.Square, accum_out=ssq[:, 1:2]), reads=[bp1], writes=[bjt, bq])
            for (j, p0, bp0, p1, bp1) in items:
                ssq = stat[:, j, 8:10]; s1 = stat[:, j, 10:11]
                S.op("dve", lambda e, ssq=ssq, s1=s1: e.tensor_tensor(out=s1, in0=ssq[:, 0:1], in1=ssq[:, 1:2], op=ALU.add), reads=[b_pn[j]], writes=[b_pn[j]])
            for (j, p0, bp0, p1, bp1) in items:
                s1 = stat[:, j, 10:11]; s2 = stat[:, j, 11:12]
                S.op("act", lambda e, s1=s1, s2=s2: e.activation(out=s2, in_=s1, func=AF.Ln, scale=1.0 / D, bias=EPS), reads=[b_pn[j]], writes=[b_pn[j]])
            for (j, p0, bp0, p1, bp1) in items:
                s2 = stat[:, j, 11:12]; r = stat[:, j, 12:13]
                S.op("act", lambda e, s2=s2, r=r: e.activation(out=r, in_=s2, func=AF.Exp, scale=-0.5), reads=[b_pn[j]], writes=[b_pn[j]])
            for (j, p0, bp0, p1, bp1) in items:
                r = stat[:, j, 12:13]
                bq = b_pn[j]
                for nh, (pp, bpp) in enumerate(((p0, bp0), (p1, bp1))):
                    tt_, btt = gtmp()
                    S.op("dve", lambda e, pp=pp, nh=nh, tt_=tt_, r=r: e.scalar_tensor_tensor(out=tt_[:, 0:512], in0=pp[:, :], scalar=r, in1=gpost[gi][:, nh * 512:(nh + 1) * 512], op0=ALU.mult, op1=ALU.mult), reads=[bpp, bq, b_gpost[gi]], writes=[btt])
                    S.op("dve" if nh == 0 else "pool", lambda e, j=j, nh=nh, tt_=tt_: e.tensor_tensor(out=xt[:, j, nh * 512:(nh + 1) * 512], in0=xt[:, j, nh * 512:(nh + 1) * 512], in1=tt_[:, 0:512], op=ALU.add), reads=[btt, b_xtj[j]], writes=[b_xtj[j]])

        def load_x(t_, j):
            tk0 = t_ * TT
            S.dma("sp", lambda e: e.dma_start(out=xt[:, j, :], in_=x_d[tk0 + j * 128:tk0 + (j + 1) * 128, :]),
                  b_xtj[j], reads=[], writes=[b_xtj[j], b_stg_f[j // 2]])

        def prep_tables(t_):
            tk0 = t_ * TT
            pi_t, b_posi = gtmp(); pf_t, b_posf = gtmp()
            posi = pi_t[:, 0:TT].bitcast(I32)
            posf_ = pf_t[:, 0:TT]
            S.dma("sp", lambda e: e.dma_start(out=posi, in_=pos_d[0:1, tk0:tk0 + TT].partition_broadcast(128)), b_posi, writes=[b_posi])
            S.op("dve", lambda e: e.tensor_copy(out=posf_, in_=posi), reads=[b_posi], writes=[b_posf])
            trig(0, cosA, sinA, b_tabA, 0, 128, posf_, b_posf, posi, b_posi)
            trig(1, cosB, sinB, b_tabB, 64, 96, posf_, b_posf, posi, b_posi)

        b_kc = [B("KC%d" % h) for h in range(8)]
        b_vc = [B("VC%d" % h) for h in range(8)]

        def load_kv_past(t_, h):
            if t_ == 0:
                return
            sl = h % 2
            nk = t_ * TT
            S.dma("sp", lambda e: e.dma_start(out=kbuf[sl][0:96, 0:nk], in_=KC_s[h, 0:96, 0:nk]), b_kbuf[sl], reads=[b_kc[h]], writes=[b_kbuf[sl]])
            S.dma("sp", lambda e: e.dma_start(out=vbuf[sl][:, 0:4 * t_, :], in_=VC_s[h, :, 0:4 * t_, :]), b_vbuf[sl], reads=[b_vc[h]], writes=[b_vbuf[sl]])

        def do_tile(t):
            tok0 = t * TT
            S.cur = "t%d:A" % t
            if t == 0:
                for j in range(4):
                    load_x(0, j)
            if t == 0:
                prep_tables(0)
            norm_T("attn")
            R_ACT = [b_act, b_mixed, b_kst, b_vones] + b_vst_e + b_vst_o
            W_ACT = R_ACT + b_stg_b
            X0 = b_stg_b if t == 0 else []
            if t == 0:
                dump("cosA", cosA[:], b_tabA, [128, TT]); dump("sinA", sinA[:], b_tabA, [128, TT])
                dump("cosB", cosB[64:96, :], b_tabB, [32, TT]); dump("sinB", sinB[64:96, :], b_tabB, [32, TT])
                dump("hT", hT[:], b_hT, [128, 8, TT], BF16)
            S.cur = "t%d:B" % t
            def load_win(u):
                rt, brt = load_ring(WIN_s[u].rearrange("p a k n -> p (a k n)"), "WIN%d" % u, 2048)
                return (rt[:, 0:1024].rearrange("p (k n) -> p k n", n=128), rt[:, 1024:2048].rearrange("p (k n) -> p k n", n=128), brt)

            def rope_evac(p1, bp1, p2, bp2, cos_t, sin_t, btab, out_ap, bout, r0, r1):
                t1, bt1 = gtmp(); t2, bt2 = gtmp()
                S.op("dve", lambda e: e.tensor_tensor(out=t1[r0:r1, 0:TT], in0=p1[r0:r1, :], in1=cos_t[r0:r1, :], op=ALU.mult), reads=[bp1, btab], writes=[bt1])
                S.op("dve", lambda e: e.tensor_tensor(out=t2[r0:r1, 0:TT], in0=p2[r0:r1, :], in1=sin_t[r0:r1, :], op=ALU.mult), reads=[bp2, btab], writes=[bt2])
                S.op("pool", lambda e: e.tensor_tensor(out=out_ap, in0=t1[r0:r1, 0:TT], in1=t2[r0:r1, 0:TT], op=ALU.add), reads=[bt1, bt2], writes=[bout])

            for c in range(4):
                w0, w1, bw = load_win(c)
                p1, bp1 = proj_fm(w0, bw, hT, b_hT)
                p2, bp2 = proj_fm(w1, bw, hT, b_hT)
                rope_evac(p1, bp1, p2, bp2, cosA, sinA, b_tabA, QA[:, c, :], b_QA[c], 0, 128)
            w0, w1, bw = load_win(4)
            p1, bp1 = proj_fm(w0, bw, hT, b_hT)
            p2, bp2 = proj_fm(w1, bw, hT, b_hT)
            rope_evac(p1, bp1, p2, bp2, cosA, sinA, b_tabA, KAb[:, 128:640], b_KA, 0, 128)
            w0, w1, bw = load_win(5)
            for c, wv in enumerate((w0, w1)):
                pc, bpc = proj_fm(wv, bw, hT, b_hT)
                S.op("act", lambda e, c=c, pc=pc: e.activation(out=cqb[:, c, :], in_=pc[:, :], func=AF.Identity, scale=G_Q[:, c:c + 1]), reads=[bpc, b_C], writes=[b_cqb])
                S.op("act", lambda e, c=c, pc=pc: e.activation(out=sqq[:, c, :], in_=pc[:, :], func=AF.Square), reads=[bpc], writes=[b_sqq])
            psq, bpsq = gps()
            for c in range(2):
                mm(psq[:, :], onesb[:, :], sqq[:, c, :], c == 0, c == 1, [b_ones, b_sqq], bpsq)
            tq, btq = gtmp()
            S.op("act", lambda e: e.activation(out=tq[:, 0:TT], in_=psq[:, :], func=AF.Ln, scale=1.0 / 256, bias=EPS), reads=[bpsq], writes=[btq])
            S.op("act", lambda e: e.activation(out=rstdq[:], in_=tq[:, 0:TT], func=AF.Exp, scale=-0.5), reads=[btq], writes=[b_rq])
            w0, w1, bw = load_win(6)
            pc, bpc = proj_fm(w0, bw, hT, b_hT)
            S.op("act", lambda e, pc=pc: e.activation(out=ckvb[:, :], in_=pc[:, :], func=AF.Identity, scale=G_KV[:, 0:1]), reads=[bpc, b_C], writes=[b_ckvb])
            S.op("act", lambda e, pc=pc: e.activation(out=sqkv[:, :], in_=pc[:, :], func=AF.Square), reads=[bpc], writes=[b_sqkv])
            psk, bpsk = gps()
            mm(psk[:, :], onesb[:, :], sqkv[:, :], True, True, [b_ones, b_sqkv], bpsk)
            tk, btk = gtmp()
            S.op("act", lambda e: e.activation(out=tk[:, 0:TT], in_=psk[:, :], func=AF.Ln, scale=1.0 / 128, bias=EPS), reads=[bpsk], writes=[btk])
            S.op("act", lambda e: e.activation(out=rstdkv[:], in_=tk[:, 0:TT], func=AF.Exp, scale=-0.5), reads=[btk], writes=[b_rkv])
            pst, bpst = gps()
            for j in range(4):
                mm(pst[:, j:j + 1], sqkv[:, j * 128:(j + 1) * 128], onesb[:, 0:1], True, True, [b_sqkv, b_ones], bpst)
            rkt = small[:, 25:29]; rk2 = small[:, 28:32]
            S.op("act", lambda e: e.activation(out=small[:, 0:4], in_=pst[:, 0:4], func=AF.Ln, scale=1.0 / 128, bias=EPS), reads=[bpst], writes=[b_small])
            S.op("act", lambda e: e.activation(out=rk2, in_=small[:, 0:4], func=AF.Exp, scale=-0.5), reads=[b_small], writes=[b_small])
            pv, bpv = gps()
            for j in range(4):
                for kc in range(8):
                    mm(pv[:, j * 128:(j + 1) * 128], hT[:, kc, j * 128:(j + 1) * 128], w1[:, kc, :], kc == 0, kc == 7, [b_hT, bw], bpv)
            pv3 = pv[:, :].rearrange("p (j c) -> p j c", c=128)
            S.op("act", lambda e: e.copy(out=VAb[:, 1:5, 0:64], in_=pv3[:, :, 0:64]), reads=[bpv], writes=[b_VA])
            S.op("dve", lambda e: e.tensor_copy(out=VAb[:, 1:5, 128:192], in_=pv3[:, :, 64:128]), reads=[bpv], writes=[b_VA])
            w0, w1, bw = load_win(7)
            p1, bp1 = proj_fm(w0, bw, hT, b_hT)
            p2, bp2 = proj_fm(w1, bw, hT, b_hT)
            kpe_t, bkpe = gtmp()
            kpe = kpe_t[64:96, 0:TT]
            rope_evac(p1, bp1, p2, bp2, cosB, sinB, b_tabB, kpe, bkpe, 64, 96)
            for h in range(8):
                en = ("pool", "dve")[h % 2]
                S.op("dve", lambda e, h=h: e.tensor_copy(out=kst[64:96, h, :], in_=kpe), reads=[bkpe], writes=[b_kst] + X0)
            S.op("dve", lambda e: e.tensor_tensor(out=cosB[64:96, :], in0=cosB[64:96, :], in1=rstdq[64:96, :], op=ALU.mult), reads=[b_rq, b_tabB], writes=[b_tabB])
            S.op("dve", lambda e: e.tensor_tensor(out=sinB[64:96, :], in0=sinB[64:96, :], in1=rstdq[64:96, :], op=ALU.mult), reads=[b_rq, b_tabB], writes=[b_tabB])
            S.cur = "t%d:C" % t
            rt, brt = load_ring(WUKV_s.rearrange("p a n -> p (a n)"), "WUKV", 1024)
            wukv = rt
            for h in range(8):
                pk, bpk = gps()
                mm(pk[0:64, :], wukv[:, h * 64:(h + 1) * 64], ckvb[:, :], True, True, [brt, b_ckvb], bpk)
                S.op("dve", lambda e, h=h, pk=pk: e.tensor_tensor(out=kst[0:64, h, :], in0=pk[0:64, :], in1=rstdkv[0:64, :], op=ALU.mult), reads=[bpk, b_rkv], writes=[b_kst] + X0)
            S.op("pool", lambda e: e.memset(vst[:, :, 0:8:2, 64:128], 1.0), writes=[b_vones] + X0)
            S.op("pool", lambda e: e.memset(vst[:, :, 1:8:2, 0:64], 1.0), writes=[b_vones])
            for j in range(4):
                pvb, bpvb = gps()
                mm(pvb[:, :], ckvb[:, j * 128:(j + 1) * 128], wukv[:, 512:1024], True, True, [b_ckvb, brt], bpvb)
                pv4 = pvb[:, :].rearrange("p (h d) -> p h d", d=64)
                S.op("act", lambda e, j=j, pv4=pv4: e.activation(out=vst[:, j, 0:8:2, 0:64], in_=pv4[:, 0:8:2, :], func=AF.Identity, scale=rk2[:, j:j + 1]), reads=[bpvb, b_small], writes=[b_vst_e[j]] + X0)
                S.op("dve", lambda e, j=j, pv4=pv4: e.tensor_scalar(out=vst[:, j, 1:8:2, 64:128], in0=pv4[:, 1:8:2, :], scalar1=rk2[:, j:j + 1], scalar2=None, op0=ALU.mult), reads=[bpvb, b_small, b_vst_e[j]], writes=[b_vst_o[j]] + X0)
            for h in range(8):
                S.dma("sp", lambda e, h=h, tok0=tok0: e.dma_start(out=KC_s[h, 0:96, tok0:tok0 + TT], in_=kst[0:96, h, :]), b_kc[h], reads=[b_kst], writes=[b_kc[h]])
                S.dma("sp", lambda e, h=h, t=t: e.dma_start(out=VC_s[h, :, 4 * t:4 * t + 4, :], in_=vst[:, :, h, :]), b_vc[h], reads=[b_vones] + b_vst_e + b_vst_o, writes=[b_vc[h]])

            if t == 0:
                dump("QA", QA[:], b_QA[3], [128, 4, TT], BF16); dump("KAb", KAb[:], b_KA, [128, 640], BF16); dump("VAb", VAb[:], b_VA, [128, 5, 192], BF16)
                dump("sqq", sqq[:], b_sqq, [128, 2, TT], BF16); dump("ckvb", ckvb[:], b_ckvb, [128, TT], BF16); dump("sqkv", sqkv[:], b_sqkv, [128, TT], BF16); dump("small", small[:], b_small, [128, 32])
                dump("cqb", cqb[:], b_cqb, [128, 2, TT], BF16); dump("rstdq", rstdq[:], b_rq, [128, TT]); dump("rstdkv", rstdkv[:], b_rkv, [128, TT])
                dump("kst", kst, b_kst, [128, 8, TT], BF16); dump("vst", act[:, 8:16, :], b_vst_o[3], [128, 8, TT], BF16)

            def load_kv(h):
                load_kv_past(t, h)
            ruq0, bruq0 = load_ring(WUQ_s[0].rearrange("p k n -> p (k n)"), "WUQ", 1536)
            ruq1, bruq1 = load_ring(WUQ_s[1].rearrange("p k n -> p (k n)"), "WUQ", 1536)
            wuq = [ruq0[:, 0:1536].rearrange("p (k n) -> p k n", n=768), ruq1[:, 0:1536].rearrange("p (k n) -> p k n", n=768)]

            S.cur = "t%d:D1" % t
            LA = 3

            deferred = []

            def tick():
                for d_ in deferred:
                    d_[0] -= 1
                while deferred and deferred[0][0] <= 0:
                    deferred.pop(0)[1]()

            def run_pipeline(items):
                pend = []
                for p1, p2 in items:
                    ctx = p1()
                    pend.append((p2, ctx))
                    if len(pend) > LA:
                        f_, c_ = pend.pop(0)
                        f_(c_)
                        tick()
                for f_, c_ in pend:
                    f_(c_)
                    tick()
                while deferred:
                    deferred.pop(0)[1]()

            swa_items = []
            swa_po = {}
            swa_octr = [0]
            for c in range(4):
                for half in range(2):
                    jjs = [jj for jj in range(5) if not (t == 0 and jj == 0)]
                    for jj in jjs:
                        def p1(c=c, half=half, jj=jj, jjs=jjs):
                            r0 = half * 64
                            if jj == jjs[0]:
                                bk = 6 + (swa_octr[0] % 2)
                                swa_octr[0] += 1
                                swa_po[(c, half)] = (PS[bk], b_ps[bk])
                            qb0 = max(jj - 1, 0); qb1 = min(jj, 3)
                            ncol = (qb1 - qb0 + 1) * 128
                            ps_, bps_ = gps()
                            if jj == 0:
                                mk = maskbias[:, 128:256]
                            elif jj == 4:
                                mk = maskbias[:, 0:128]
                            else:
                                mk = maskbias[:, 0:256]
                            mm(ps_[:, 0:ncol], KAb[r0:r0 + 64, jj * 128:(jj + 1) * 128], QA[r0:r0 + 64, c, qb0 * 128:qb0 * 128 + ncol], True, False, [b_KA, b_QA[c]], bps_)
                            mm(ps_[:, 0:ncol], identb[:, :], mk, False, True, [b_id, b_mask], bps_)
                            pt_, bpt_ = gpt()
                            S.op("act", lambda e: e.activation(out=pt_[:, 0:ncol], in_=ps_[:, 0:ncol], func=AF.Exp, scale=SC_A), reads=[bps_], writes=[bpt_])
                            return (pt_, bpt_, qb0, qb1)

                        def p2(ctx, c=c, half=half, jj=jj, jjs=jjs):
                            pt_, bpt_, qb0, qb1 = ctx
                            r0 = half * 64
                            h = c + 4 * half
                            voff = 0 if half == 0 else 64
                            po, bpo = swa_po[(c, half)]
                            for qb in range(qb0, qb1 + 1):
                                first = (jj == qb) if not (t == 0 and qb == 0) else (jj == 1)
                                last = (jj == qb + 1)
                                lc = (qb - qb0) * 128
                                mm(po[:, qb * 128:(qb + 1) * 128], VAb[:, jj, voff:voff + 128], pt_[:, lc:lc + 128], first, last, [b_VA, bpt_], bpo)
                            if jj == jjs[-1]:
                                def fin(po=po, bpo=bpo, r0=r0, h=h, c=c):
                                    rO = slice(r0, r0 + 64); rZ = slice(64 - r0, 128 - r0)
                                    tz, btz = gtmp(); rz, brz = gtmp()
                                    if t <= 1:
                                        S.op("act", lambda e: e.activation(out=tz[rZ, 0:TT], in_=po[rZ, :], func=AF.Ln, bias=esink[rZ, h:h + 1]), reads=[bpo, b_esink], writes=[btz])
                                        S.op("act", lambda e: e.activation(out=rz[rO, 0:TT], in_=tz[rZ, 0:TT], func=AF.Exp, scale=-1.0), reads=[btz], writes=[brz])
                                    else:
                                        S.op("dve", lambda e: e.tensor_scalar(out=tz[rZ, 0:TT], in0=po[rZ, :], scalar1=esink[rZ, h:h + 1], scalar2=None, op0=ALU.add), reads=[bpo, b_esink], writes=[btz])
                                        S.op("dve", lambda e: e.reciprocal(out=rz[rO, 0:TT], in_=tz[rZ, 0:TT]), reads=[btz], writes=[brz])
                                    S.op("dve", lambda e: e.tensor_tensor(out=ya[rO, c, :], in0=po[rO, :], in1=rz[rO, 0:TT], op=ALU.mult), reads=[bpo, brz], writes=[b_ya])
                                deferred.append([3, fin])
                        swa_items.append((p1, p2))

            S.cur = "t%d:D2" % t
            ring_banks[0] = (0, 1, 2, 3)
            nkc = 4 * (t + 1)
            mla_po = {}
            mla_items = []

            def emit_q(h):
                sl = h % 2
                pq1, bpq1 = gps()
                for k2 in range(2):
                    mm(pq1[0:96, :], wuq[0][:, k2, h * 96:(h + 1) * 96], cqb[:, k2, :], k2 == 0, k2 == 1, [bruq0, b_cqb], bpq1)
                pq2, bpq2 = gps()
                for k2 in range(2):
                    mm(pq2[0:96, :], wuq[1][:, k2, h * 96:(h + 1) * 96], cqb[:, k2, :], k2 == 0, k2 == 1, [bruq1, b_cqb], bpq2)
                S.op("dve", lambda e: e.tensor_tensor(out=QB[0:64, sl, :], in0=pq1[0:64, :], in1=rstdq[0:64, :], op=ALU.mult), reads=[bpq1, b_rq], writes=[b_QB[sl]])
                rope_evac(pq1, bpq1, pq2, bpq2, cosB, sinB, b_tabB, QB[64:96, sl, :], b_QB[sl], 64, 96)

            for h in range(8):
                for kc in range(nkc):
                    def p1(h=h, kc=kc):
                        sl = h % 2
                        if kc == 0 and h == 0:
                            emit_q(0)
                        if kc == 0:
                            mla_po[h] = ops_()
                        if kc == 1 and h + 1 < 8:
                            emit_q(h + 1)
                        jd = kc - 4 * t
                        q0 = 0 if jd <= 0 else jd * 128
                        ncol = TT - q0
                        ps_, bps_ = gps()
                        if jd >= 0:
                            kk = kst[0:96, h, jd * 128:(jd + 1) * 128]
                            mm(ps_[:, 0:128], kk, QB[0:96, sl, q0:q0 + 128], True, False, [b_kst, b_QB[sl]], bps_)
                            mm(ps_[:, 0:128], identb[:, :], maskbias[:, 0:128], False, True, [b_id, b_mask], bps_)
                            if ncol > 128:
                                mm(ps_[:, 128:ncol], kk, QB[0:96, sl, q0 + 128:TT], True, True, [b_kst, b_QB[sl]], bps_)
                        else:
                            kk = kbuf[sl][0:96, kc * 128:(kc + 1) * 128]
                            mm(ps_[:, 0:ncol], kk, QB[0:96, sl, q0:TT], True, True, [b_kbuf[sl], b_QB[sl]], bps_)
                        pt_, bpt_ = gpt()
                        S.op("act", lambda e: e.activation(out=pt_[:, 0:ncol], in_=ps_[:, 0:ncol], func=AF.Exp, scale=SC_B), reads=[bps_], writes=[bpt_])
                        return (pt_, bpt_, q0, ncol)

                    def p2(ctx, h=h, kc=kc):
                        pt_, bpt_, q0, ncol = ctx
                        sl = h % 2
                        po, bpo = mla_po[h]
                        jd2 = kc - 4 * t
                        if jd2 >= 0:
                            mm(po[:, q0:TT], vst[:, jd2, h, :], pt_[:, 0:ncol], kc == 0, kc == nkc - 1, [b_vst_e[jd2], b_vst_o[jd2], b_vones, bpt_], bpo)
                        else:
                            mm(po[:, q0:TT], vbuf[sl][:, kc, :], pt_[:, 0:ncol], kc == 0, kc == nkc - 1, [b_vbuf[sl], bpt_], bpo)
                        if kc == nkc - 1:
                            def fin(po=po, bpo=bpo, h=h):
                                r0 = 0 if h % 2 == 0 else 64
                                rO = slice(r0, r0 + 64); rZ = slice(64 - r0, 128 - r0)
                                tz, btz = gtmp(); rz, brz = gtmp()
                                if t <= 1:
                                    S.op("act", lambda e: e.activation(out=tz[rZ, 0:TT], in_=po[rZ, :], func=AF.Ln), reads=[bpo], writes=[btz])
                                    S.op("act", lambda e: e.activation(out=rz[rO, 0:TT], in_=tz[rZ, 0:TT], func=AF.Exp, scale=-1.0), reads=[btz], writes=[brz])
                                else:
                                    S.op("dve", lambda e: e.reciprocal(out=rz[rO, 0:TT], in_=po[rZ, :]), reads=[bpo], writes=[brz])
                                S.op("dve", lambda e: e.tensor_tensor(out=yb[rO, h // 2, :], in0=po[rO, :], in1=rz[rO, 0:TT], op=ALU.mult), reads=[bpo, brz], writes=[b_yb])
                            deferred.append([3, fin])
                            if h + 2 < 8:
                                load_kv(h + 2)
                    mla_items.append((p1, p2))
            merged = []
            n_m, n_s = len(mla_items), len(swa_items)
            si = 0
            for mi, it in enumerate(mla_items):
                merged.append(it)
                while si < n_s and (si + 1) * n_m <= (mi + 1) * n_s:
                    merged.append(swa_items[si]); si += 1
            merged.extend(swa_items[si:])
            run_pipeline(merged)
            S.op("pool", lambda e: e.tensor_copy(out=KAb[:, 0:128], in_=KAb[:, 512:640]), reads=[b_KA], writes=[b_KA])
            S.op("pool", lambda e: e.tensor_copy(out=VAb[:, 0, :], in_=VAb[:, 4, :]), reads=[b_VA], writes=[b_VA])

            ring_banks[0] = (0, 1, 2, 3)
            if t == 0:
                dump("ya", ya[:], b_ya, [128, 4, TT], BF16); dump("yb", yb[:], b_yb, [128, 4, TT], BF16)
            S.cur = "t%d:E" % t
            for n in range(8):
                rg, brg = load_ring(WG_s[n].rearrange("p k n -> p (k n)"), "WG%d" % n, 2048)
                wg = rg[:, 0:2048].rearrange("p (k n) -> p k n", n=256)
                rab, brab = load_ring(WAB_s[n].rearrange("p k n -> p (k n)"), "WAB%d" % n, 1024)
                wab = rab[:, 0:1024].rearrange("p (k n) -> p k n", n=128)
                gts = []
                for gi_ in range(2):
                    pg, bpg = gps()
                    for kc in range(8):
                        mm(pg[:, :], wg[:, kc, gi_ * 128:(gi_ + 1) * 128], hT[:, kc, :], kc == 0, kc == 7, [brg, b_hT], bpg)
                    gt, bgt = gtmp()
                    S.op("act", lambda e, pg=pg, gt=gt, gi_=gi_, n=n: e.activation(out=gt[:, 0:TT], in_=pg[:, :], func=AF.Sigmoid, bias=BG[:, gi_ * 8 + n:gi_ * 8 + n + 1]), reads=[bpg, b_C], writes=[bgt])
                    gts.append((gt, bgt))
                for gi_, (yy, byy) in enumerate(((ya, b_ya), (yb, b_yb))):
                    pa, bpa = gps()
                    for kc in range(4):
                        mm(pa[:, :], wab[:, gi_ * 4 + kc, :], yy[:, kc, :], kc == 0, kc == 3, [brab, byy], bpa)
                    gt, bgt = gts[gi_]
                    S.op("dve", lambda e, pa=pa, gt=gt: e.tensor_tensor(out=gt[:, 0:TT], in0=gt[:, 0:TT], in1=pa[:, :], op=ALU.mult), reads=[bpa, bgt], writes=[bgt])
                S.op("pool", lambda e, n=n, g0=gts[0][0], g1=gts[1][0]: e.tensor_tensor(out=mixed[:, n, :], in0=g0[:, 0:TT], in1=g1[:, 0:TT], op=ALU.add), reads=[gts[0][1], gts[1][1]], writes=[b_mixed, b_kst])

            if t + 1 < NT:
                load_kv_past(t + 1, 0)
                load_kv_past(t + 1, 1)
            S.cur = "t%d:F" % t
            for nh in range(2):
                load_big(nh, WOUT_s[nh].rearrange("p k n -> p (k n)"), "WOUT%d" % nh, 4096)
            wo = [big[nh][:, 0:4096].rearrange("p (k n) -> p k n", n=512) for nh in range(2)]
            for jp in range(2):
                items = []
                for j in (2 * jp, 2 * jp + 1):
                    pp = []
                    for nh in range(2):
                        ps_, bps_ = gps()
                        for kc in range(8):
                            mm(ps_[:, :], mixed[:, kc, j * 128:(j + 1) * 128], wo[nh][:, kc, :], kc == 0, kc == 7, [b_mixed, b_big[nh]], bps_)
                        pp.append((ps_, bps_))
                    items.append((j, pp[0][0], pp[0][1], pp[1][0], pp[1][1]))
                post_norm_residual_multi(items, 0)

            if t == 0:
                dump("mixed", mixed[:], b_mixed, [128, 8, TT], BF16); dump("x1", xt[:], b_xtj[3], [128, 4, D])
            S.cur = "t%d:G" % t
            norm_T("mlp")
            for u_ in range(2):
                load_big(u_, WDN_s[u_].rearrange("p k n -> p (k n)"), "WDN%d" % u_, 5632)
            ring_banks[0] = (0, 1, 2, 3, 4, 5, 6, 7)
            pend_fin = []
            for f in range(22):
                ru, bru = load_ring(WUP_s[f].rearrange("p a k n -> p (a k n)"), "WUP%d" % f, 2048)
                ys = []
                for gv in range(2):
                    wv = ru[:, gv * 1024:(gv + 1) * 1024].rearrange("p (k n) -> p k n", n=128)
                    pu, bpu = proj_fm(wv, bru, hT, b_hT)
                    idx = gv * 22 + f
                    y, by = gtmp()
                    z, bz = gtmp()
                    bh = b_halo_i[idx]
                    S.op("act", lambda e, y=y, pu=pu, idx=idx: e.activation(out=y[:, 0:TT], in_=pu[:, :], func=AF.Identity, scale=CW[2][:, idx:idx + 1], bias=CBIAS[:, idx:idx + 1]), reads=[bpu, b_C], writes=[by])
                    S.op("act", lambda e, z=z, pu=pu, idx=idx: e.activation(out=z[:, 2:TT], in_=pu[:, 0:TT - 2], func=AF.Identity, scale=CW[0][:, idx:idx + 1]), reads=[bpu, b_C], writes=[bz])
                    S.op("act", lambda e, z=z, idx=idx: e.activation(out=z[:, 0:2], in_=halo[:, idx, 0:2], func=AF.Identity, scale=CW[0][:, idx:idx + 1]), reads=[bh, b_C], writes=[bz])
                    S.op("dve", lambda e, y=y, idx=idx: e.scalar_tensor_tensor(out=y[:, 0:1], in0=halo[:, idx, 1:2], scalar=CW[1][:, idx:idx + 1], in1=y[:, 0:1], op0=ALU.mult, op1=ALU.add), reads=[bh, by, bz, b_C], writes=[by])
                    S.op("dve", lambda e, pu=pu, idx=idx: e.tensor_copy(out=halo[:, idx, :], in_=pu[:, TT - 2:TT]), reads=[bpu, bz], writes=[bh])
                    S.op("dve", lambda e, y=y, pu=pu, idx=idx: e.scalar_tensor_tensor(out=y[:, 1:TT], in0=pu[:, 0:TT - 1], scalar=CW[1][:, idx:idx + 1], in1=y[:, 1:TT], op0=ALU.mult, op1=ALU.add), reads=[bpu, by, bz, b_C], writes=[by])
                    en = "dve" if gv == 0 else "pool"
                    S.op(en, lambda e, y=y, z=z: e.tensor_tensor(out=y[:, 0:TT], in0=y[:, 0:TT], in1=z[:, 0:TT], op=ALU.add), reads=[by, bz], writes=[by])
                    ys.append((y, by))
                def fin(f=f, ys=ys):
                    (yg, byg), (yv, byv) = ys
                    S.op("act", lambda e: e.activation(out=yg[:, 0:TT], in_=yg[:, 0:TT], func=AF.Gelu_apprx_tanh), reads=[byg], writes=[byg])
                    S.op("pool", lambda e: e.tensor_tensor(out=act[:, f, :], in0=yg[:, 0:TT], in1=yv[:, 0:TT], op=ALU.mult), reads=[byg, byv], writes=W_ACT)
                if pend_fin:
                    pend_fin.pop(0)()
                pend_fin.append(fin)
            while pend_fin:
                pend_fin.pop(0)()
            ring_banks[0] = (0, 1, 2, 3)
            if t + 1 < NT:
                S.cur = "t%d:A" % (t + 1)
                prep_tables(t + 1)
            S.cur = "t%d:G2" % t
            for nh in range(2):
                for fh in range(2):
                    u = nh * 2 + fh
                    if u >= 2:
                        load_big(u % 2, WDN_s[u].rearrange("p k n -> p (k n)"), "WDN%d" % u, 5632)
                    wd = big[u % 2][:, 0:5632].rearrange("p (k n) -> p k n", n=512)
                    for j in range(4):
                        bk = nh * 4 + j
                        for fi in range(11):
                            f = fh * 11 + fi
                            mm(PS[bk][:, :], act[:, f, j * 128:(j + 1) * 128], wd[:, fi, :], f == 0, f == 21, R_ACT + [b_big[u % 2]], b_ps[bk])
            post_norm_residual_multi([(j, PS[j], b_ps[j], PS[4 + j], b_ps[4 + j]) for j in range(4)], 1)

            if t == 0:
                dump("act", act[:], b_act, [128, 22, TT], BF16); dump("x2", xt[:], b_xtj[3], [128, 4, D])
            S.cur = "t%d:H" % t
            norm_T("ple")
            pf_, bpf_ = gtmp(); pf2, bpf2 = gtmp()
            S.dma("sp", lambda e, tok0=tok0, pf_=pf_, pf2=pf2: e.dma_start(out=pf_[:, 0:512].rearrange("p (j d) -> p j d", d=256), in_=p_d[tok0:tok0 + 256, :].rearrange("(j p) d -> p j d", p=128)), bpf_, writes=[bpf_])
            S.dma("sp", lambda e, tok0=tok0, pf2=pf2: e.dma_start(out=pf2[:, 0:512].rearrange("p (j d) -> p j d", d=256), in_=p_d[tok0 + 256:tok0 + 512, :].rearrange("(j p) d -> p j d", p=128)), bpf2, writes=[bpf2])
            S.op("pool", lambda e, pf_=pf_: e.tensor_copy(out=hn[:, 0:512], in_=pf_[:, 0:512]), reads=[bpf_], writes=[b_hn])
            S.op("pool", lambda e, pf2=pf2: e.tensor_copy(out=hn[:, 512:1024], in_=pf2[:, 0:512]), reads=[bpf2], writes=[b_hn])
            pb, pf, bp = tps()
            for j in range(4):
                for c in range(2):
                    S.op("pe", lambda e, j=j, c=c, pb=pb: e.transpose(out=pb[:, c * 512 + j * 128:c * 512 + (j + 1) * 128], in_=hn[:, j * 256 + c * 128:j * 256 + (c + 1) * 128], identity=identb[:]), reads=[b_hn, b_id], writes=[bp])
            S.op("act", lambda e, pb=pb: e.copy(out=ptT[:, :, :], in_=pb[:, 0:1024].rearrange("p (c t) -> p c t", t=512)), reads=[bp], writes=[b_ptT])
            rp, brp = load_ring(WPLE_s.rearrange("p k n -> p (k n)"), "WPLE", 2048)
            wple = rp[:, 0:2048].rearrange("p (k n) -> p k n", n=1024)
            for nh in range(2):
                load_big(nh, WPG_s[nh].rearrange("p k n -> p (k n)"), "WPG%d" % nh, 4096)
            wpg = [big[nh][:, 0:4096].rearrange("p (k n) -> p k n", n=512) for nh in range(2)]
            for j in range(4):
                for nh in range(2):
                    pg, bpg = gps()
                    for kc in range(8):
                        mm(pg[:, :], hT[:, kc, j * 128:(j + 1) * 128], wpg[nh][:, kc, :], kc == 0, kc == 7, [b_hT, b_big[nh]], bpg)
                    pe_, bpe_ = gps()
                    for kc in range(2):
                        mm(pe_[:, :], ptT[:, kc, j * 128:(j + 1) * 128], wple[:, kc, nh * 512:(nh + 1) * 512], kc == 0, kc == 1, [b_ptT, brp], bpe_)
                    sg, bsg = gtmp()
                    S.op("act", lambda e, pg=pg, sg=sg: e.activation(out=sg[:, 0:TT], in_=pg[:, :], func=AF.Sigmoid), reads=[bpg], writes=[bsg])
                    S.op("dve", lambda e, pe_=pe_, sg=sg: e.tensor_tensor(out=sg[:, 0:TT], in0=sg[:, 0:TT], in1=pe_[:, :], op=ALU.mult), reads=[bpe_, bsg], writes=[bsg])
                    S.op("pool", lambda e, j=j, nh=nh, sg=sg: e.tensor_tensor(out=xt[:, j, nh * 512:(nh + 1) * 512], in0=xt[:, j, nh * 512:(nh + 1) * 512], in1=sg[:, 0:TT], op=ALU.add), reads=[bsg, b_xtj[j]], writes=[b_xtj[j]])
                S.dma("sp", lambda e, tok0=tok0, j=j: e.dma_start(out=out_d[tok0 + j * 128:tok0 + (j + 1) * 128, :], in_=xt[:, j, :]), b_xtj[j], reads=[b_xtj[j]])
                if t + 1 < NT:
                    load_x(t + 1, j)

        for t_ in range(NT):
            do_tile(t_)

        S.emit()
        print("sched stats", S.stats, flush=True)
        import os as _os
        if _os.environ.get("KLABELS"):
            import json as _json
            _json.dump(S.labels, open(_os.environ["KLABELS"], "w"))
    return nc


_CACHE = {}
_DBG = False
_LAST = {}


def _consts():
    c = np.zeros((128, 4), np.float32)
    p = np.arange(128)
    invA = (10000.0 ** (-(np.arange(0, 64, 2, dtype=np.float32) / 64))).astype(np.float32)
    invB = (10000.0 ** (-(np.arange(0, 32, 2, dtype=np.float32) / 32))).astype(np.float32)
    c[:, 0] = invA[p % 32]
    c[:, 1] = invB[p % 16]
    return c


def kernel(**inputs):
    x = np.asarray(inputs["x"])
    Bn, SEQ, _ = x.shape
    key = SEQ
    if key not in _CACHE:
        _CACHE[key] = build_program(SEQ, dbg=_DBG)
    nc = _CACHE[key]
    shared = {"cst": _consts()}
    for k in ("w_in", "w_uq", "w_ukv", "w_branch_a", "w_branch_b", "w_out", "w_up", "w_down", "w_ple_gate", "w_ple", "conv_w"):
        shared[k] = np.ascontiguousarray(np.asarray(inputs[k])[0], dtype=np.float32)
    for k in ("conv_b", "b_gate", "sinks", "attn_pre_norm", "attn_post_norm", "mlp_pre_norm", "mlp_post_norm", "ple_norm", "q_a_norm", "kv_a_norm"):
        shared[k] = np.ascontiguousarray(np.asarray(inputs[k]), dtype=np.float32).reshape(1, -1)
    p = np.asarray(inputs["p"])[0]
    pos = np.asarray(inputs["positions"]).astype(np.int32)
    in_maps = []
    for b in range(Bn):
        m = dict(shared)
        m["x"] = np.ascontiguousarray(x[b], dtype=np.float32)
        m["p"] = np.ascontiguousarray(p[b], dtype=np.float32)
        m["pos"] = np.ascontiguousarray(pos[b].reshape(1, SEQ))
        in_maps.append(m)
    res = run_bass_kernel_spmd(nc, in_maps, core_ids=list(range(Bn)))
    if _DBG:
        _LAST.update(res.results[0])
    out = np.stack([np.asarray(r["out"], dtype=np.float32) for r in res.results], axis=0)
    return out
```

```python
import numpy as np
from concourse.bass_utils import run_bass_kernel_spmd
from contextlib import ExitStack
import concourse.bass as bass
import concourse.mybir as mybir

F32 = mybir.dt.float32
BF16 = mybir.dt.bfloat16
I32 = mybir.dt.int32
AF = mybir.ActivationFunctionType
ALU = mybir.AluOpType
AX = mybir.AxisListType


class Buf:
    __slots__ = ("name", "last_w", "readers", "sem", "ndma")

    def __init__(self, name):
        self.name = name
        self.last_w = None
        self.readers = []
        self.sem = None
        self.ndma = 0


class Ins:
    __slots__ = ("eng", "fn", "deps", "dma", "home", "dval", "signal", "sigidx", "snap", "waits", "stage")

    def __init__(self, eng, fn, dma=False, home=None):
        self.eng = eng
        self.fn = fn
        self.deps = []
        self.dma = dma
        self.home = home
        self.dval = 0
        self.signal = False
        self.sigidx = 0
        self.snap = None
        self.waits = None


class Sched:
    ENGS = ("pe", "act", "dve", "pool", "sp")

    def __init__(self, nc, stack):
        self.nc = nc
        self.stack = stack
        self.prog = []
        self.esem = {e: stack.enter_context(nc.semaphore("s_" + e)) for e in self.ENGS}
        self.dma_bufs = []
        self.nbuf = 0
        self.cur = "init"

    def sbuf(self, name, shape, dt):
        return self.stack.enter_context(self.nc.sbuf_tensor(name, list(shape), dt))

    def psum(self, name, shape, dt):
        return self.stack.enter_context(self.nc.psum_tensor(name, list(shape), dt))

    def buf(self, name=None):
        self.nbuf += 1
        return Buf(name or ("b%d" % self.nbuf))

    def _add(self, ins, reads, writes):
        deps = []
        for b in reads:
            if b.last_w is not None:
                deps.append(b.last_w)
        for b in writes:
            if b.last_w is not None:
                deps.append(b.last_w)
            deps.extend(b.readers)
        for d in deps:
            if d is ins:
                continue
            if not d.dma and not ins.dma and d.eng == ins.eng:
                if ins.eng == "pe":
                    continue
            ins.deps.append(d)
        for b in reads:
            b.readers.append(ins)
        for b in writes:
            b.last_w = ins
            b.readers = []
        ins.stage = self.cur
        self.prog.append(ins)
        return ins

    def op(self, eng, fn, reads=(), writes=()):
        ins = Ins(eng, fn)
        deps_raw = set()
        for b in reads:
            if b.last_w is not None:
                deps_raw.add(id(b.last_w))
        self._add(ins, reads, writes)
        if eng != "pool":
            ins.deps = [d for d in ins.deps if d.dma or d.eng != eng or id(d) in deps_raw]
        for d in ins.deps:
            if not d.dma:
                d.signal = True
        return ins

    def dma(self, eng, fn, home, reads=(), writes=()):
        if home.sem is None:
            home.sem = self.stack.enter_context(self.nc.semaphore("d_" + home.name))
            self.dma_bufs.append(home)
        ins = Ins(eng, fn, dma=True, home=home)
        self._add(ins, reads, writes)
        home.ndma += 1
        ins.dval = 16 * home.ndma
        for d in ins.deps:
            if not d.dma:
                d.signal = True
        return ins

    def emit(self):
        nc = self.nc
        known = {e: {} for e in self.ENGS}
        sigcnt = {e: 0 for e in self.ENGS}
        streams = {e: [] for e in self.ENGS}
        issued = {}
        nwaits = 0
        for ins in self.prog:
            e = ins.eng
            kn = known[e]
            need = {}
            for d in ins.deps:
                if d.dma:
                    sem = d.home.sem
                    key = ("d", id(d.home))
                    val = 16 * issued[id(d.home)]
                else:
                    sem = self.esem[d.eng]
                    key = ("e", d.eng)
                    val = d.sigidx
                    assert val > 0
                if kn.get(key, 0) >= val:
                    continue
                if key not in need or need[key][1] < val:
                    need[key] = (sem, val, d)
            for key, (sem, val, d) in need.items():
                if kn.get(key, 0) >= val:
                    continue
                streams[e].append(("w", sem, val))
                nwaits += 1
                kn[key] = val
                if not d.dma and d.snap is not None:
                    for k2, v2 in d.snap.items():
                        if kn.get(k2, 0) < v2:
                            kn[k2] = v2
            if ins.dma:
                issued[id(ins.home)] = issued.get(id(ins.home), 0) + 1
            else:
                if ins.signal:
                    sigcnt[e] += 1
                    ins.sigidx = sigcnt[e]
                    ins.snap = dict(kn)
                    ins.snap[("e", e)] = ins.sigidx
            streams[e].append(("i", ins))
        for b in self.dma_bufs:
            streams["sp"].append(("w", b.sem, 16 * b.ndma))
        self.stats = dict(n_ins=len(self.prog), n_waits=nwaits,
                          per_eng={e: len(streams[e]) for e in self.ENGS},
                          nsem=len(self.dma_bufs) + 5)
        esem = self.esem
        self.labels = {e: [it[1].stage for it in streams[e] if it[0] == "i" and not it[1].dma] for e in self.ENGS}

        def run(engobj, lst, e):
            for item in lst:
                if item[0] == "w":
                    engobj.wait_ge(item[1], item[2])
                else:
                    ins = item[1]
                    bi = ins.fn(engobj)
                    if ins.dma:
                        bi.then_inc(ins.home.sem, 16)
                    elif ins.signal:
                        bi.then_inc(esem[e], 1)

        with nc.Block() as block:
            @block.sync
            def _(eng):
                run(eng, streams["sp"], "sp")

            @block.tensor
            def _(eng):
                run(eng, streams["pe"], "pe")

            @block.scalar
            def _(eng):
                run(eng, streams["act"], "act")

            @block.vector
            def _(eng):
                run(eng, streams["dve"], "dve")

            @block.gpsimd
            def _(eng):
                run(eng, streams["pool"], "pool")


D = 1024
TT = 512
EPS = 1e-6
SC_A = 64 ** -0.5
SC_B = 96 ** -0.5
C1_2PI = 6.28125
C2_2PI = 2 * np.pi - 6.28125


def build_program(SEQ, dbg=False):
    NT = SEQ // TT
    NBLK = SEQ // 128
    nc = bass.Bass("TRN2", target_bir_lowering=False)

    def din(name, shape, dt=F32):
        return nc.dram_tensor(name, list(shape), dt, kind="ExternalInput").ap()

    x_d = din("x", [SEQ, D])
    p_d = din("p", [SEQ, 256])
    pos_d = din("pos", [1, SEQ], I32)
    cst_d = din("cst", [128, 4])
    w_in_d = din("w_in", [D, 3232])
    w_uq_d = din("w_uq", [256, 768])
    w_ukv_d = din("w_ukv", [128, 1024])
    w_a_d = din("w_branch_a", [512, D])
    w_b_d = din("w_branch_b", [512, D])
    w_out_d = din("w_out", [D, D])
    w_up_d = din("w_up", [D, 5632])
    w_dn_d = din("w_down", [2816, D])
    w_pg_d = din("w_ple_gate", [D, D])
    w_ple_d = din("w_ple", [256, D])
    conv_w_d = din("conv_w", [3, 5632])
    conv_b_d = din("conv_b", [1, 5632])
    b_gate_d = din("b_gate", [1, 2048])
    sinks_d = din("sinks", [1, 8])
    g_attn_pre_d = din("attn_pre_norm", [1, D])
    g_attn_post_d = din("attn_post_norm", [1, D])
    g_mlp_pre_d = din("mlp_pre_norm", [1, D])
    g_mlp_post_d = din("mlp_post_norm", [1, D])
    g_ple_d = din("ple_norm", [1, D])
    g_q_d = din("q_a_norm", [1, 256])
    g_kv_d = din("kv_a_norm", [1, 128])
    out_d = nc.dram_tensor("out", [SEQ, D], F32, kind="ExternalOutput").ap()

    def scr(name, shape):
        return nc.dram_tensor(name, list(shape), BF16, kind="Internal").ap()

    WIN_s = scr("WIN_s", [8, 128, 2, 8, 128])
    WUQ_s = scr("WUQ_s", [2, 128, 2, 768])
    WUKV_s = scr("WUKV_s", [128, 1, 1024])
    WG_s = scr("WG_s", [8, 128, 8, 256])
    WAB_s = scr("WAB_s", [8, 128, 8, 128])
    WOUT_s = scr("WOUT_s", [2, 128, 8, 512])
    WUP_s = scr("WUP_s", [22, 128, 2, 8, 128])
    WDN_s = scr("WDN_s", [4, 128, 11, 512])
    WPG_s = scr("WPG_s", [2, 128, 8, 512])
    WPLE_s = scr("WPLE_s", [128, 2, 1024])
    KC_s = scr("KC_s", [8, 128, SEQ])
    VC_s = scr("VC_s", [8, 128, NBLK, 128])

    with ExitStack() as st:
        S = Sched(nc, st)
        sb = S.sbuf
        B = S.buf

        xt = sb("xt", [128, 4, D], F32); b_xtj = [B("xt%d" % j) for j in range(4)]
        hn = sb("hn", [128, D], BF16); b_hn = B("hn")
        hn2 = [hn, sb("hn_b", [128, D], BF16)]; b_hn2 = [b_hn, B("hn_b")]
        stat = sb("stat", [128, 4, 16], F32); b_stat = [B("stat%d" % j) for j in range(4)]; b_pn = [B("pn%d" % j) for j in range(4)]
        hT = sb("hT", [128, 8, TT], BF16); b_hT = B("hT")
        QA = sb("QA", [128, 4, TT], BF16); b_QA = [B() for _ in range(4)]
        KAb = sb("KAb", [128, 640], BF16); b_KA = B("KA")
        VAb = sb("VAb", [128, 5, 192], BF16); b_VA = B("VA")
        cqb = sb("cqb", [128, 2, TT], BF16); b_cqb = B()
        sqq = sb("sqq", [128, 2, TT], BF16); b_sqq = B()
        ckvb = sb("ckvb", [128, TT], BF16); b_ckvb = B()
        sqkv = sb("sqkv", [128, TT], BF16); b_sqkv = B()
        rstdq = sb("rstdq", [128, TT], F32); b_rq = B()
        rstdkv = sb("rstdkv", [128, TT], F32); b_rkv = B()
        cosA = sb("cosA", [128, TT], F32); sinA = sb("sinA", [128, TT], F32); b_tabA = B()
        cosB = sb("cosB", [128, TT], F32); sinB = sb("sinB", [128, TT], F32); b_tabB = B()
        NTMP = 12
        tmpf = [sb("tmpf%d" % i, [128, 514], F32) for i in range(NTMP)]
        b_tmp = [B("tmp%d" % i) for i in range(NTMP)]
        QB = sb("QB", [128, 2, TT], BF16); b_QB = [B(), B()]
        NPT = 5
        ptb = [sb("pt%d" % i, [128, TT], BF16) for i in range(NPT)]; b_pt = [B() for _ in range(NPT)]
        ya = sb("ya", [128, 4, TT], BF16); b_ya = B("ya")
        yb = sb("yb", [128, 4, TT], BF16); b_yb = B("yb")
        b_mixed = B("mixed")
        kbuf = [sb("kbuf%d" % i, [128, SEQ], BF16) for i in range(2)]; b_kbuf = [B("kbuf0"), B("kbuf1")]
        vbuf = [sb("vbuf%d" % i, [128, NBLK, 128], BF16) for i in range(2)]; b_vbuf = [B("vbuf0"), B("vbuf1")]
        act = sb("act", [128, 22, TT], BF16); b_act = B("act")
        b_kst = B("kst"); b_vones = B("vones"); b_vst_e = [B("vste%d" % j) for j in range(4)]; b_vst_o = [B("vsto%d" % j) for j in range(4)]
        kst = act[:, 0:8, :]
        mixed = act[:, 0:8, :]
        vst = act[:, 8:16, :].rearrange("p a (b c) -> p (a b) c", c=128).rearrange("p (j h) c -> p j h c", h=8)
        gpost = [sb("gpost%d" % i, [128, D], F32) for i in range(2)]; b_gpost = [B("gpost0"), B("gpost1")]
        NRING = 4
        ring = [sb("ring%d" % i, [128, 2048], BF16) for i in range(NRING)]; b_ring = [B("ring%d" % i) for i in range(NRING)]
        big = [sb("big%d" % i, [128, 5632], BF16) for i in range(2)]; b_big = [B("big0"), B("big1")]
        ptT = sb("ptT", [128, 2, TT], BF16); b_ptT = B()
        halo = sb("halo", [128, 44, 2], F32); b_halo = B("halo"); b_halo_i = [B("halo%d" % i) for i in range(44)]
        R1 = sb("R1", [128, 128], F32); R2 = sb("R2", [128, 128], F32); b_R = B()
        C1 = sb("C1", [128, 128], F32); C2 = sb("C2", [128, 128], F32); b_C = B()
        identf = sb("identf", [128, 128], F32); identb = sb("identb", [128, 128], BF16); b_id = B()
        onesf = sb("onesf", [128, 128], F32); onesb = sb("onesb", [128, 128], BF16); b_ones = B()
        maskf = sb("maskf", [128, 256], F32); maskb = sb("maskb", [128, 256], BF16); b_mask = B(); maskbias = sb("maskbias", [128, 256], BF16)
        esink = sb("esink", [128, 8], F32); b_esink = B()
        cst = sb("cst_sb", [128, 4], F32); b_cst = B()
        small = sb("small", [128, 32], F32); b_small = B("small")
        negones = None

        PS = [S.psum("ps%d" % i, [128, 512], F32) for i in range(8)]
        b_ps = [B("ps%d" % i) for i in range(8)]
        PSB = [p[:].bitcast(BF16) for p in PS]
        ring_ctr = [0]

        ring_banks = [(0, 1, 2, 3)]

        def gps():
            rb = ring_banks[0]
            i = rb[ring_ctr[0] % len(rb)]
            ring_ctr[0] += 1
            return PS[i], b_ps[i]
        o_ctr = [0]

        def ops_():
            i = 4 + (o_ctr[0] % 2)
            o_ctr[0] += 1
            return PS[i], b_ps[i]
        t_ctr = [0]

        def tps():
            i = 6 + (t_ctr[0] % 2)
            t_ctr[0] += 1
            return PSB[i], PS[i], b_ps[i]
        tmp_ctr = [0]

        def gtmp():
            i = tmp_ctr[0] % NTMP
            tmp_ctr[0] += 1
            return tmpf[i], b_tmp[i]
        pt_ctr = [0]

        def gpt():
            i = pt_ctr[0] % NPT
            pt_ctr[0] += 1
            return ptb[i], b_pt[i]
        rr_ctr = [0]

        def gring():
            i = rr_ctr[0] % NRING
            rr_ctr[0] += 1
            return ring[i], b_ring[i]
        ew_ctr = [0]

        def ew():
            ew_ctr[0] += 1
            return "dve" if ew_ctr[0] % 2 else "pool"

        DBG = {}

        def dump(name, ap, buf, shape, dt=F32):
            if not dbg:
                return
            d = nc.dram_tensor("dbg_" + name, list(shape), dt, kind="ExternalOutput").ap()
            DBG[name] = d
            S.dma("sp", lambda e: e.dma_start(out=d, in_=ap), buf, reads=[buf])

        S.dma("sp", lambda e: e.dma_start(out=cst[:], in_=cst_d[:, :]), b_cst, writes=[b_cst])
        S.op("pool", lambda e: e.memset(onesf[:], 1.0), writes=[b_ones])
        S.op("pool", lambda e: e.memset(onesb[:], 1.0), writes=[b_ones])
        S.op("pool", lambda e: e.affine_select(out=identf[:], in_=onesf[:], pattern=[[-1, 128]], compare_op=ALU.is_equal, fill=0.0, base=0, channel_multiplier=1), reads=[b_ones], writes=[b_id])
        S.op("pool", lambda e: e.tensor_copy(out=identb[:], in_=identf[:]), reads=[b_id], writes=[b_id])
        S.op("pool", lambda e: e.affine_select(out=maskf[:, 0:128], in_=onesf[:], pattern=[[1, 128]], compare_op=ALU.is_ge, fill=0.0, base=0, channel_multiplier=-1), reads=[b_ones], writes=[b_mask])
        S.op("pool", lambda e: e.affine_select(out=maskf[:, 128:256], in_=onesf[:], pattern=[[-1, 128]], compare_op=ALU.is_ge, fill=0.0, base=-1, channel_multiplier=1), reads=[b_ones], writes=[b_mask])
        S.op("pool", lambda e: e.tensor_copy(out=maskb[:], in_=maskf[:]), reads=[b_mask], writes=[b_mask])
        S.op("pool", lambda e: e.tensor_scalar(out=maskbias[:], in0=maskf[:], scalar1=-1.0, scalar2=30000.0, op0=ALU.add, op1=ALU.mult), reads=[b_mask], writes=[b_mask])
        S.op("pool", lambda e: e.memset(halo[:], 0.0), writes=[b_halo] + b_halo_i)
        S.op("pool", lambda e: e.memset(KAb[:], 0.0), writes=[b_KA])
        S.op("pool", lambda e: e.memset(VAb[:], 0.0), writes=[b_VA])
        S.op("pool", lambda e: e.memset(VAb[:, :, 64:128], 1.0), writes=[b_VA])
        S.op("pool", lambda e: e.memset(R1[:], 0.0), writes=[b_R])
        S.op("pool", lambda e: e.memset(R2[:], 0.0), writes=[b_R])
        def rload(Rt, r0, n, src):
            S.dma("sp", lambda e: e.dma_start(out=Rt[r0:r0 + n, :], in_=src.rearrange("(i p) -> i p", p=128)), b_R, writes=[b_R])
        rload(R1, 0, 44, conv_w_d[0, :]); rload(R1, 44, 44, conv_w_d[1, :])
        rload(R1, 88, 2, g_q_d[0, :]); rload(R1, 90, 1, g_kv_d[0, :])
        rload(R2, 0, 44, conv_w_d[2, :]); rload(R2, 44, 44, conv_b_d[0, :]); rload(R2, 88, 16, b_gate_d[0, :])
        rload(R2, 104, 8, g_attn_pre_d[0, :]); rload(R2, 112, 8, g_mlp_pre_d[0, :]); rload(R2, 120, 8, g_ple_d[0, :])
        for Rt, Ct, bank in ((R1, C1, 0), (R2, C2, 1)):
            S.op("pe", lambda e, Rt=Rt, bank=bank: e.transpose(out=PS[bank][:, 0:128], in_=Rt[:], identity=identf[:]), reads=[b_R, b_id], writes=[b_ps[bank]])
            S.op("dve", lambda e, Ct=Ct, bank=bank: e.tensor_copy(out=Ct[:], in_=PS[bank][:, 0:128]), reads=[b_ps[bank]], writes=[b_C])
        CW = [C1[:, 0:44], C1[:, 44:88], C2[:, 0:44]]
        CBIAS = C2[:, 44:88]
        BG = C2[:, 88:104]
        G_Q = C1[:, 88:90]
        G_KV = C1[:, 90:91]
        G_PRE = {"attn": C2[:, 104:112], "mlp": C2[:, 112:120], "ple": C2[:, 120:128]}
        S.dma("sp", lambda e: e.dma_start(out=gpost[0][:], in_=g_attn_post_d[0:1, :].partition_broadcast(128)), b_gpost[0], writes=[b_gpost[0]])
        S.dma("sp", lambda e: e.dma_start(out=gpost[1][:], in_=g_mlp_post_d[0:1, :].partition_broadcast(128)), b_gpost[1], writes=[b_gpost[1]])
        S.dma("sp", lambda e: e.dma_start(out=small[:, 0:8], in_=sinks_d[0:1, :].partition_broadcast(128)), b_small, writes=[b_small])
        S.op("act", lambda e: e.activation(out=esink[:], in_=small[:, 0:8], func=AF.Exp), reads=[b_small], writes=[b_esink])

        dump("C1", C1[:], b_C, [128, 128]); dump("C2", C2[:], b_C, [128, 128]); dump("esink", esink[:], b_esink, [128, 8])
        dump("maskb", maskb[:], b_mask, [128, 256], BF16)
        S.cur = "prepass"
        b_scr = {}

        def sbuf_of(name):
            if name not in b_scr:
                b_scr[name] = B("scr_" + name)
            return b_scr[name]
        pp_ctr = [0]
        stg_f = [xt[:, 0:2, :].rearrange("p a b -> p (a b)"), xt[:, 2:4, :].rearrange("p a b -> p (a b)")]
        b_stg_f = [B("stgf0"), B("stgf1")]
        if SEQ >= 4096:
            stg_f += [kbuf[0][:, :].bitcast(F32), kbuf[1][:, :].bitcast(F32), vbuf[0][:].rearrange("p a b -> p (a b)").bitcast(F32)]
            b_stg_f += [b_kbuf[0], b_kbuf[1], b_vbuf[0]]
        NSTG = len(stg_f)
        act_flat = act[:].rearrange("p a b -> p (a b)")
        stg_b = [act_flat[:, i * 2048:(i + 1) * 2048] for i in range(NSTG)]
        b_stg_b = [B("stgb%d" % i) for i in range(NSTG)]
        cast_ctr = [0]

        def cast_eng():
            cast_ctr[0] += 1
            return ("dve", "act")[cast_ctr[0] % 2]

        def piece(dst, scr_name, KC, N, loads, g=None, negs=(), zero=False, cast_fn=None, rot=None):
            i = pp_ctr[0] % NSTG
            pp_ctr[0] += 1
            sf = stg_f[i][:, 0:KC * N].rearrange("p (k n) -> p k n", n=N)
            sbv = stg_b[i][:, 0:KC * N].rearrange("p (k n) -> p k n", n=N)
            bf_, bb_ = b_stg_f[i], b_stg_b[i]
            if zero:
                S.op("pool", lambda e: e.memset(sf, 0.0), writes=[bf_])
            for ld in loads:
                if len(ld) == 3:
                    c0, n, src = ld
                    S.dma("sp", lambda e, c0=c0, n=n, src=src: e.dma_start(out=sf[:, :, c0:c0 + n], in_=src), bf_, writes=[bf_])
                elif len(ld) == 2:
                    dfn, src = ld
                    S.dma("sp", lambda e, dfn=dfn, src=src: e.dma_start(out=dfn(sf), in_=src), bf_, writes=[bf_])
                else:
                    c0, n, src, p0, p1, k0 = ld
                    S.dma("sp", lambda e, c0=c0, n=n, src=src, p0=p0, p1=p1, k0=k0: e.dma_start(out=sf[p0:p1, k0, c0:c0 + n], in_=src), bf_, writes=[bf_])
            if cast_fn is not None:
                cast_fn(sf, sbv, [bf_], [bb_])
            elif g is None:
                en = cast_eng()
                if en == "act":
                    S.op("act", lambda e: e.copy(out=sbv, in_=sf), reads=[bf_], writes=[bb_])
                else:
                    S.op(en, lambda e: e.tensor_copy(out=sbv, in_=sf), reads=[bf_], writes=[bb_])
            else:
                for kc in range(KC):
                    en = cast_eng()
                    if en == "act":
                        S.op("act", lambda e, kc=kc: e.activation(out=sbv[:, kc, :], in_=sf[:, kc, :], func=AF.Identity, scale=g[:, kc:kc + 1]), reads=[bf_, b_C], writes=[bb_])
                    else:
                        S.op(en, lambda e, kc=kc: e.tensor_scalar(out=sbv[:, kc, :], in0=sf[:, kc, :], scalar1=g[:, kc:kc + 1], scalar2=None, op0=ALU.mult), reads=[bf_, b_C], writes=[bb_])
            for (c0, n) in negs:
                S.op("dve", lambda e, c0=c0, n=n: e.tensor_scalar(out=sbv[:, :, c0:c0 + n], in0=sbv[:, :, c0:c0 + n], scalar1=-1.0, scalar2=None, op0=ALU.mult), reads=[bb_], writes=[bb_])
            bs = sbuf_of(scr_name)
            S.dma("pool", lambda e: e.dma_start(out=dst, in_=sbv), bb_, reads=[bb_], writes=[bs])
            if rot is not None:
                dst_rot, blocks, zero_rot = rot
                i2 = pp_ctr[0] % NSTG
                pp_ctr[0] += 1
                sb2 = stg_b[i2][:, 0:KC * N].rearrange("p (k n) -> p k n", n=N)
                bb2 = b_stg_b[i2]
                if zero_rot:
                    S.op("pool", lambda e: e.memset(sb2, 0.0), writes=[bb2])
                for (b0, hd) in blocks:
                    hh = hd // 2
                    S.op("dve", lambda e, b0=b0, hh=hh: e.tensor_scalar(out=sb2[:, :, b0:b0 + hh], in0=sbv[:, :, b0 + hh:b0 + 2 * hh], scalar1=-1.0, scalar2=None, op0=ALU.mult), reads=[bb_], writes=[bb2])
                    S.op("act", lambda e, b0=b0, hh=hh: e.copy(out=sb2[:, :, b0 + hh:b0 + 2 * hh], in_=sbv[:, :, b0:b0 + hh]), reads=[bb_], writes=[bb2])
                S.dma("pool", lambda e: e.dma_start(out=dst_rot, in_=sb2), bb2, reads=[bb2], writes=[bs])

        def wk(w, c0, n):
            return w[:, c0:c0 + n].rearrange("(k p) n -> p k n", p=128)

        gpre = G_PRE["attn"]
        def win_dst(u, sl):
            return WIN_s[u, :, sl, :, :]
        for c in range(4):
            ba, bb2 = c * 64, (c + 4) * 64
            piece(win_dst(c, 0), "WIN%d" % c, 8, 128, [(0, 64, wk(w_in_d, ba, 64)), (64, 64, wk(w_in_d, bb2, 64))], g=gpre,
                  rot=(win_dst(c, 1), [(0, 64), (64, 64)], False))
        piece(win_dst(4, 0), "WIN4", 8, 128, [(0, 128, wk(w_in_d, 512, 128))], g=gpre, rot=(win_dst(4, 1), [(0, 64), (64, 64)], False))
        piece(win_dst(5, 0), "WIN5", 8, 128, [(0, 128, wk(w_in_d, 768, 128))], g=gpre)
        piece(win_dst(5, 1), "WIN5", 8, 128, [(0, 128, wk(w_in_d, 896, 128))], g=gpre)
        piece(win_dst(6, 0), "WIN6", 8, 128, [(0, 128, wk(w_in_d, 1024, 128))], g=gpre)
        piece(win_dst(6, 1), "WIN6", 8, 128, [(0, 128, wk(w_in_d, 640, 128))], g=gpre)
        piece(win_dst(7, 0), "WIN7", 8, 128, [(64, 32, wk(w_in_d, 1152, 32))], g=gpre, zero=True, rot=(win_dst(7, 1), [(64, 32)], True))
        def uq_rot_cast(sf, sbv, rd, wr):
            sf4 = sf.rearrange("p k (h c) -> p k h c", c=96)
            sb4 = sbv.rearrange("p k (h c) -> p k h c", c=96)
            S.op("pool", lambda e: e.memset(sbv, 0.0), writes=wr)
            S.op("dve", lambda e: e.tensor_scalar(out=sb4[:, :, :, 64:80], in0=sf4[:, :, :, 80:96], scalar1=-1.0, scalar2=None, op0=ALU.mult), reads=rd, writes=wr)
            S.op("dve", lambda e: e.tensor_copy(out=sb4[:, :, :, 80:96], in_=sf4[:, :, :, 64:80]), reads=rd, writes=wr)
        piece(WUQ_s[0], "WUQ", 2, 768, [(0, 768, wk(w_uq_d, 0, 768))])
        piece(WUQ_s[1], "WUQ", 2, 768, [(0, 768, wk(w_uq_d, 0, 768))], cast_fn=uq_rot_cast)

        def ukv_cast(sf, sbv, rd, wr):
            src = sf[:, 0, :].rearrange("p (h two d) -> p two h d", two=2, d=64)
            dstv = sbv[:, 0, :].rearrange("p (two h d) -> p two h d", two=2, d=64)
            S.op("dve", lambda e: e.tensor_copy(out=dstv[:, 0], in_=src[:, 0]), reads=rd, writes=wr)
            S.op("act", lambda e: e.copy(out=dstv[:, 1], in_=src[:, 1]), reads=rd, writes=wr)
        piece(WUKV_s, "WUKV", 1, 1024, [(0, 1024, wk(w_ukv_d, 0, 1024))], cast_fn=ukv_cast)
        for n in range(8):
            piece(WG_s[n], "WG%d" % n, 8, 256, [(0, 128, wk(w_in_d, 1184 + 128 * n, 128)), (128, 128, wk(w_in_d, 2208 + 128 * n, 128))], g=gpre)
        for n in range(8):
            cs = slice(n * 128, (n + 1) * 128)
            lds = [(lambda sf: sf[0:64, 0:4, :], w_a_d[0:256, cs].rearrange("(c p) n -> p c n", p=64)),
                   (lambda sf: sf[64:128, 0:4, :], w_a_d[256:512, cs].rearrange("(c p) n -> p c n", p=64)),
                   (lambda sf: sf[:, 4:8, :], w_b_d[:, cs].rearrange("(k p) n -> p k n", p=128))]
            piece(WAB_s[n], "WAB%d" % n, 8, 128, lds)
        for nh in range(2):
            for kh in range(2):
                piece(WOUT_s[nh, :, kh * 4:(kh + 1) * 4, :], "WOUT%d" % nh, 4, 512, [(0, 512, w_out_d[kh * 512:(kh + 1) * 512, nh * 512:(nh + 1) * 512].rearrange("(k p) n -> p k n", p=128))])
        gm = G_PRE["mlp"]
        for f in range(22):
            for gv in range(2):
                piece(WUP_s[f, :, gv, :, :], "WUP%d" % f, 8, 128, [(0, 128, wk(w_up_d, gv * 2816 + f * 128, 128))], g=gm)
        for nh in range(2):
            for fh in range(2):
                u = nh * 2 + fh
                for (f0, nf) in ((0, 4), (4, 4), (8, 3)):
                    r0 = (fh * 11 + f0) * 128
                    piece(WDN_s[u, :, f0:f0 + nf, :], "WDN%d" % u, nf, 512, [(0, 512, w_dn_d[r0:r0 + nf * 128, nh * 512:(nh + 1) * 512].rearrange("(k p) n -> p k n", p=128))])
        gp = G_PRE["ple"]
        for nh in range(2):
            for kh in range(2):
                piece(WPG_s[nh, :, kh * 4:(kh + 1) * 4, :], "WPG%d" % nh, 4, 512, [(0, 512, w_pg_d[kh * 512:(kh + 1) * 512, nh * 512:(nh + 1) * 512].rearrange("(k p) n -> p k n", p=128))], g=gp[:, kh * 4:(kh + 1) * 4])
        piece(WPLE_s, "WPLE", 2, 1024, [(0, 1024, wk(w_ple_d, 0, 1024))])

        def load_ring(src, scr_name, nelem):
            rt, brt = gring()
            v = rt[:, 0:nelem]
            S.dma("sp", lambda e: e.dma_start(out=v, in_=src), brt, reads=[sbuf_of(scr_name)], writes=[brt])
            return rt, brt

        def load_big(i, src, scr_name, nelem):
            v = big[i][:, 0:nelem]
            S.dma("sp", lambda e: e.dma_start(out=v, in_=src), b_big[i], reads=[sbuf_of(scr_name)], writes=[b_big[i]])

        def mm(ps, lhsT, rhs, start, stop, reads, bps):
            S.op("pe", lambda e: e.matmul(ps, lhsT=lhsT, rhs=rhs, start=start, stop=stop, skip_group_check=True), reads=reads, writes=[bps])

        def norm_T(tag):
            for j in range(4):
                ssq = stat[:, j, 0:1]; rt_ = stat[:, j, 1:2]; rs = stat[:, j, 2:3]
                bs_ = b_stat[j]
                jt, bjt = gtmp()
                junk = jt[:, 0:512].bitcast(BF16)
                S.op("pool", lambda e, ssq=ssq: e.memset(ssq, 0.0), writes=[bs_])
                S.op("act", lambda e, j=j, junk=junk, ssq=ssq: e.activation(out=junk, in_=xt[:, j, :], func=AF.Square, accum_out=ssq), reads=[b_xtj[j]], writes=[bjt, bs_])
                S.op("act", lambda e, rt_=rt_, ssq=ssq: e.activation(out=rt_, in_=ssq, func=AF.Ln, scale=1.0 / D, bias=EPS), reads=[bs_], writes=[bs_])
                S.op("act", lambda e, rs=rs, rt_=rt_: e.activation(out=rs, in_=rt_, func=AF.Exp, scale=-0.5), reads=[bs_], writes=[bs_])
            for j in range(4):
                rs = stat[:, j, 2:3]
                bs_ = b_stat[j]
                hnb = hn2[j % 2]; bhnb = b_hn2[j % 2]
                S.op("dve", lambda e, j=j, hnb=hnb, rs=rs: e.tensor_scalar(out=hnb[:], in0=xt[:, j, :], scalar1=rs, scalar2=None, op0=ALU.mult), reads=[b_xtj[j], bs_], writes=[bhnb])
                pb, pf, bp = tps()
                for c in range(8):
                    S.op("pe", lambda e, c=c, pb=pb, hnb=hnb: e.transpose(out=pb[:, c * 128:(c + 1) * 128], in_=hnb[:, c * 128:(c + 1) * 128], identity=identb[:]), reads=[bhnb, b_id], writes=[bp])
                S.op("act", lambda e, j=j, pb=pb: e.copy(out=hT[:, :, j * 128:(j + 1) * 128], in_=pb[:, 0:1024].rearrange("p (c t) -> p c t", t=128)), reads=[bp], writes=[b_hT])

        def proj_fm(wv, bw, rhsT, brhs, KC=8):
            ps, bps = gps()
            for kc in range(KC):
                mm(ps[:, :], wv[:, kc, :], rhsT[:, kc, :], kc == 0, kc == KC - 1, [bw, brhs], bps)
            return ps, bps

        def trig(inv_col, cos_t, sin_t, btab, p0, p1, posf_, b_posf, posi, b_posi):
            a_t, ba = gtmp(); k_t, bk = gtmp()
            ang = a_t[p0:p1, 0:TT]
            kf = k_t[p0:p1, 0:TT]
            ki = posi[p0:p1, :]
            for (shift, outt) in ((0.0, sin_t), (np.pi / 2, cos_t)):
                S.op("dve", lambda e, shift=shift: e.tensor_scalar(out=ang, in0=posf_[p0:p1, :], scalar1=cst[p0:p1, inv_col:inv_col + 1], scalar2=shift, op0=ALU.mult, op1=ALU.add), reads=[b_posf, b_cst], writes=[ba])
                S.op("dve", lambda e: e.tensor_scalar(out=kf, in0=ang, scalar1=float(1.0 / (2 * np.pi)), scalar2=None, op0=ALU.mult), reads=[ba], writes=[bk])
                S.op("dve", lambda e: e.tensor_copy(out=ki, in_=kf), reads=[bk], writes=[b_posi])
                S.op("dve", lambda e: e.tensor_copy(out=kf, in_=ki), reads=[b_posi], writes=[bk])
                S.op("dve", lambda e: e.scalar_tensor_tensor(out=ang, in0=kf, scalar=-C1_2PI, in1=ang, op0=ALU.mult, op1=ALU.add), reads=[bk, ba], writes=[ba])
                S.op("dve", lambda e: e.scalar_tensor_tensor(out=ang, in0=kf, scalar=-C2_2PI, in1=ang, op0=ALU.mult, op1=ALU.add), reads=[bk, ba], writes=[ba])
                S.op("dve", lambda e: e.tensor_scalar(out=ang, in0=ang, scalar1=-3.1415925, scalar2=3.1415925, op0=ALU.max, op1=ALU.min), reads=[ba], writes=[ba])
                S.op("act", lambda e, outt=outt: e.activation(out=outt[p0:p1, :], in_=ang, func=AF.Sin), reads=[ba], writes=[btab])

        def post_norm_residual_multi(items, gi):
            junks = []
            for (j, p0, bp0, p1, bp1) in items:
                ssq = stat[:, j, 8:10]
                bq = b_pn[j]
                jt, bjt = gtmp()
                junk = jt[:, 0:256].bitcast(BF16)
                S.op("pool", lambda e, ssq=ssq: e.memset(ssq, 0.0), writes=[bq])
                S.op("act", lambda e, junk=junk, p0=p0, ssq=ssq: e.activation(out=junk, in_=p0[:, :], func=AF.Square, accum_out=ssq[:, 0:1]), reads=[bp0], writes=[bjt, bq])
                S.op("act", lambda e, junk=junk, p1=p1, ssq=ssq: e.activation(out=junk, in_=p1[:, :], func=AF.Square, accum_out=ssq[:, 1:2]), reads=[bp1], writes=[bjt, bq])
            for (j, p0, bp0, p1, bp1) in items:
                ssq = stat[:, j, 8:10]; s1 = stat[:, j, 10:11]
                S.op("dve", lambda e, ssq=ssq, s1=s1: e.tensor_tensor(out=s1, in0=ssq[:, 0:1], in1=ssq[:, 1:2], op=ALU.add), reads=[b_pn[j]], writes=[b_pn[j]])
            for (j, p0, bp0, p1, bp1) in items:
                s1 = stat[:, j, 10:11]; s2 = stat[:, j, 11:12]
                S.op("act", lambda e, s1=s1, s2=s2: e.activation(out=s2, in_=s1, func=AF.Ln, scale=1.0 / D, bias=EPS), reads=[b_pn[j]], writes=[b_pn[j]])
            for (j, p0, bp0, p1, bp1) in items:
                s2 = stat[:, j, 11:12]; r = stat[:, j, 12:13]
                S.op("act", lambda e, s2=s2, r=r: e.activation(out=r, in_=s2, func=AF.Exp, scale=-0.5), reads=[b_pn[j]], writes=[b_pn[j]])
            for (j, p0, bp0, p1, bp1) in items:
                r = stat[:, j, 12:13]
                bq = b_pn[j]
                for nh, (pp, bpp) in enumerate(((p0, bp0), (p1, bp1))):
                    tt_, btt = gtmp()
                    S.op("dve", lambda e, pp=pp, nh=nh, tt_=tt_, r=r: e.scalar_tensor_tensor(out=tt_[:, 0:512], in0=pp[:, :], scalar=r, in1=gpost[gi][:, nh * 512:(nh + 1) * 512], op0=ALU.mult, op1=ALU.mult), reads=[bpp, bq, b_gpost[gi]], writes=[btt])
                    S.op("dve" if nh == 0 else "pool", lambda e, j=j, nh=nh, tt_=tt_: e.tensor_tensor(out=xt[:, j, nh * 512:(nh + 1) * 512], in0=xt[:, j, nh * 512:(nh + 1) * 512], in1=tt_[:, 0:512], op=ALU.add), reads=[btt, b_xtj[j]], writes=[b_xtj[j]])

        def load_x(t_, j):
            tk0 = t_ * TT
            S.dma("sp", lambda e: e.dma_start(out=xt[:, j, :], in_=x_d[tk0 + j * 128:tk0 + (j + 1) * 128, :]),
                  b_xtj[j], reads=[], writes=[b_xtj[j], b_stg_f[j // 2]])

        def prep_tables(t_):
            tk0 = t_ * TT
            pi_t, b_posi = gtmp(); pf_t, b_posf = gtmp()
            posi = pi_t[:, 0:TT].bitcast(I32)
            posf_ = pf_t[:, 0:TT]
            S.dma("sp", lambda e: e.dma_start(out=posi, in_=pos_d[0:1, tk0:tk0 + TT].partition_broadcast(128)), b_posi, writes=[b_posi])
            S.op("dve", lambda e: e.tensor_copy(out=posf_, in_=posi), reads=[b_posi], writes=[b_posf])
            trig(0, cosA, sinA, b_tabA, 0, 128, posf_, b_posf, posi, b_posi)
            trig(1, cosB, sinB, b_tabB, 64, 96, posf_, b_posf, posi, b_posi)

        b_kc = [B("KC%d" % h) for h in range(8)]
        b_vc = [B("VC%d" % h) for h in range(8)]

        def load_kv_past(t_, h):
            if t_ == 0:
                return
            sl = h % 2
            nk = t_ * TT
            S.dma("sp", lambda e: e.dma_start(out=kbuf[sl][0:96, 0:nk], in_=KC_s[h, 0:96, 0:nk]), b_kbuf[sl], reads=[b_kc[h]], writes=[b_kbuf[sl]])
            S.dma("sp", lambda e: e.dma_start(out=vbuf[sl][:, 0:4 * t_, :], in_=VC_s[h, :, 0:4 * t_, :]), b_vbuf[sl], reads=[b_vc[h]], writes=[b_vbuf[sl]])

        def do_tile(t):
            tok0 = t * TT
            S.cur = "t%d:A" % t
            if t == 0:
                for j in range(4):
                    load_x(0, j)
            if t == 0:
                prep_tables(0)
            norm_T("attn")
            R_ACT = [b_act, b_mixed, b_kst, b_vones] + b_vst_e + b_vst_o
            W_ACT = R_ACT + b_stg_b
            X0 = b_stg_b if t == 0 else []
            if t == 0:
                dump("cosA", cosA[:], b_tabA, [128, TT]); dump("sinA", sinA[:], b_tabA, [128, TT])
                dump("cosB", cosB[64:96, :], b_tabB, [32, TT]); dump("sinB", sinB[64:96, :], b_tabB, [32, TT])
                dump("hT", hT[:], b_hT, [128, 8, TT], BF16)
            S.cur = "t%d:B" % t
            def load_win(u):
                rt, brt = load_ring(WIN_s[u].rearrange("p a k n -> p (a k n)"), "WIN%d" % u, 2048)
                return (rt[:, 0:1024].rearrange("p (k n) -> p k n", n=128), rt[:, 1024:2048].rearrange("p (k n) -> p k n", n=128), brt)

            def rope_evac(p1, bp1, p2, bp2, cos_t, sin_t, btab, out_ap, bout, r0, r1):
                t1, bt1 = gtmp(); t2, bt2 = gtmp()
                S.op("dve", lambda e: e.tensor_tensor(out=t1[r0:r1, 0:TT], in0=p1[r0:r1, :], in1=cos_t[r0:r1, :], op=ALU.mult), reads=[bp1, btab], writes=[bt1])
                S.op("dve", lambda e: e.tensor_tensor(out=t2[r0:r1, 0:TT], in0=p2[r0:r1, :], in1=sin_t[r0:r1, :], op=ALU.mult), reads=[bp2, btab], writes=[bt2])
                S.op("pool", lambda e: e.tensor_tensor(out=out_ap, in0=t1[r0:r1, 0:TT], in1=t2[r0:r1, 0:TT], op=ALU.add), reads=[bt1, bt2], writes=[bout])

            for c in range(4):
                w0, w1, bw = load_win(c)
                p1, bp1 = proj_fm(w0, bw, hT, b_hT)
                p2, bp2 = proj_fm(w1, bw, hT, b_hT)
                rope_evac(p1, bp1, p2, bp2, cosA, sinA, b_tabA, QA[:, c, :], b_QA[c], 0, 128)
            w0, w1, bw = load_win(4)
            p1, bp1 = proj_fm(w0, bw, hT, b_hT)
            p2, bp2 = proj_fm(w1, bw, hT, b_hT)
            rope_evac(p1, bp1, p2, bp2, cosA, sinA, b_tabA, KAb[:, 128:640], b_KA, 0, 128)
            w0, w1, bw = load_win(5)
            for c, wv in enumerate((w0, w1)):
                pc, bpc = proj_fm(wv, bw, hT, b_hT)
                S.op("act", lambda e, c=c, pc=pc: e.activation(out=cqb[:, c, :], in_=pc[:, :], func=AF.Identity, scale=G_Q[:, c:c + 1]), reads=[bpc, b_C], writes=[b_cqb])
                S.op("act", lambda e, c=c, pc=pc: e.activation(out=sqq[:, c, :], in_=pc[:, :], func=AF.Square), reads=[bpc], writes=[b_sqq])
            psq, bpsq = gps()
            for c in range(2):
                mm(psq[:, :], onesb[:, :], sqq[:, c, :], c == 0, c == 1, [b_ones, b_sqq], bpsq)
            tq, btq = gtmp()
            S.op("act", lambda e: e.activation(out=tq[:, 0:TT], in_=psq[:, :], func=AF.Ln, scale=1.0 / 256, bias=EPS), reads=[bpsq], writes=[btq])
            S.op("act", lambda e: e.activation(out=rstdq[:], in_=tq[:, 0:TT], func=AF.Exp, scale=-0.5), reads=[btq], writes=[b_rq])
            w0, w1, bw = load_win(6)
            pc, bpc = proj_fm(w0, bw, hT, b_hT)
            S.op("act", lambda e, pc=pc: e.activation(out=ckvb[:, :], in_=pc[:, :], func=AF.Identity, scale=G_KV[:, 0:1]), reads=[bpc, b_C], writes=[b_ckvb])
            S.op("act", lambda e, pc=pc: e.activation(out=sqkv[:, :], in_=pc[:, :], func=AF.Square), reads=[bpc], writes=[b_sqkv])
            psk, bpsk = gps()
            mm(psk[:, :], onesb[:, :], sqkv[:, :], True, True, [b_ones, b_sqkv], bpsk)
            tk, btk = gtmp()
            S.op("act", lambda e: e.activation(out=tk[:, 0:TT], in_=psk[:, :], func=AF.Ln, scale=1.0 / 128, bias=EPS), reads=[bpsk], writes=[btk])
            S.op("act", lambda e: e.activation(out=rstdkv[:], in_=tk[:, 0:TT], func=AF.Exp, scale=-0.5), reads=[btk], writes=[b_rkv])
            pst, bpst = gps()
            for j in range(4):
                mm(pst[:, j:j + 1], sqkv[:, j * 128:(j + 1) * 128], onesb[:, 0:1], True, True, [b_sqkv, b_ones], bpst)
            rkt = small[:, 25:29]; rk2 = small[:, 28:32]
            S.op("act", lambda e: e.activation(out=small[:, 0:4], in_=pst[:, 0:4], func=AF.Ln, scale=1.0 / 128, bias=EPS), reads=[bpst], writes=[b_small])
            S.op("act", lambda e: e.activation(out=rk2, in_=small[:, 0:4], func=AF.Exp, scale=-0.5), reads=[b_small], writes=[b_small])
            pv, bpv = gps()
            for j in range(4):
                for kc in range(8):
                    mm(pv[:, j * 128:(j + 1) * 128], hT[:, kc, j * 128:(j + 1) * 128], w1[:, kc, :], kc == 0, kc == 7, [b_hT, bw], bpv)
            pv3 = pv[:, :].rearrange("p (j c) -> p j c", c=128)
            S.op("act", lambda e: e.copy(out=VAb[:, 1:5, 0:64], in_=pv3[:, :, 0:64]), reads=[bpv], writes=[b_VA])
            S.op("dve", lambda e: e.tensor_copy(out=VAb[:, 1:5, 128:192], in_=pv3[:, :, 64:128]), reads=[bpv], writes=[b_VA])
            w0, w1, bw = load_win(7)
            p1, bp1 = proj_fm(w0, bw, hT, b_hT)
            p2, bp2 = proj_fm(w1, bw, hT, b_hT)
            kpe_t, bkpe = gtmp()
            kpe = kpe_t[64:96, 0:TT]
            rope_evac(p1, bp1, p2, bp2, cosB, sinB, b_tabB, kpe, bkpe, 64, 96)
            for h in range(8):
                en = ("pool", "dve")[h % 2]
                S.op("dve", lambda e, h=h: e.tensor_copy(out=kst[64:96, h, :], in_=kpe), reads=[bkpe], writes=[b_kst] + X0)
            S.op("dve", lambda e: e.tensor_tensor(out=cosB[64:96, :], in0=cosB[64:96, :], in1=rstdq[64:96, :], op=ALU.mult), reads=[b_rq, b_tabB], writes=[b_tabB])
            S.op("dve", lambda e: e.tensor_tensor(out=sinB[64:96, :], in0=sinB[64:96, :], in1=rstdq[64:96, :], op=ALU.mult), reads=[b_rq, b_tabB], writes=[b_tabB])
            S.cur = "t%d:C" % t
            rt, brt = load_ring(WUKV_s.rearrange("p a n -> p (a n)"), "WUKV", 1024)
            wukv = rt
            for h in range(8):
                pk, bpk = gps()
                mm(pk[0:64, :], wukv[:, h * 64:(h + 1) * 64], ckvb[:, :], True, True, [brt, b_ckvb], bpk)
                S.op("dve", lambda e, h=h, pk=pk: e.tensor_tensor(out=kst[0:64, h, :], in0=pk[0:64, :], in1=rstdkv[0:64, :], op=ALU.mult), reads=[bpk, b_rkv], writes=[b_kst] + X0)
            S.op("pool", lambda e: e.memset(vst[:, :, 0:8:2, 64:128], 1.0), writes=[b_vones] + X0)
            S.op("pool", lambda e: e.memset(vst[:, :, 1:8:2, 0:64], 1.0), writes=[b_vones])
            for j in range(4):
                pvb, bpvb = gps()
                mm(pvb[:, :], ckvb[:, j * 128:(j + 1) * 128], wukv[:, 512:1024], True, True, [b_ckvb, brt], bpvb)
                pv4 = pvb[:, :].rearrange("p (h d) -> p h d", d=64)
                S.op("act", lambda e, j=j, pv4=pv4: e.activation(out=vst[:, j, 0:8:2, 0:64], in_=pv4[:, 0:8:2, :], func=AF.Identity, scale=rk2[:, j:j + 1]), reads=[bpvb, b_small], writes=[b_vst_e[j]] + X0)
                S.op("dve", lambda e, j=j, pv4=pv4: e.tensor_scalar(out=vst[:, j, 1:8:2, 64:128], in0=pv4[:, 1:8:2, :], scalar1=rk2[:, j:j + 1], scalar2=None, op0=ALU.mult), reads=[bpvb, b_small, b_vst_e[j]], writes=[b_vst_o[j]] + X0)
            for h in range(8):
                S.dma("sp", lambda e, h=h, tok0=tok0: e.dma_start(out=KC_s[h, 0:96, tok0:tok0 + TT], in_=kst[0:96, h, :]), b_kc[h], reads=[b_kst], writes=[b_kc[h]])
                S.dma("sp", lambda e, h=h, t=t: e.dma_start(out=VC_s[h, :, 4 * t:4 * t + 4, :], in_=vst[:, :, h, :]), b_vc[h], reads=[b_vones] + b_vst_e + b_vst_o, writes=[b_vc[h]])

            if t == 0:
                dump("QA", QA[:], b_QA[3], [128, 4, TT], BF16); dump("KAb", KAb[:], b_KA, [128, 640], BF16); dump("VAb", VAb[:], b_VA, [128, 5, 192], BF16)
                dump("sqq", sqq[:], b_sqq, [128, 2, TT], BF16); dump("ckvb", ckvb[:], b_ckvb, [128, TT], BF16); dump("sqkv", sqkv[:], b_sqkv, [128, TT], BF16); dump("small", small[:], b_small, [128, 32])
                dump("cqb", cqb[:], b_cqb, [128, 2, TT], BF16); dump("rstdq", rstdq[:], b_rq, [128, TT]); dump("rstdkv", rstdkv[:], b_rkv, [128, TT])
                dump("kst", kst, b_kst, [128, 8, TT], BF16); dump("vst", act[:, 8:16, :], b_vst_o[3], [128, 8, TT], BF16)

            def load_kv(h):
                load_kv_past(t, h)
            ruq0, bruq0 = load_ring(WUQ_s[0].rearrange("p k n -> p (k n)"), "WUQ", 1536)
            ruq1, bruq1 = load_ring(WUQ_s[1].rearrange("p k n -> p (k n)"), "WUQ", 1536)
            wuq = [ruq0[:, 0:1536].rearrange("p (k n) -> p k n", n=768), ruq1[:, 0:1536].rearrange("p (k n) -> p k n", n=768)]

            S.cur = "t%d:D1" % t
            LA = 3

            deferred = []

            def tick():
                for d_ in deferred:
                    d_[0] -= 1
                while deferred and deferred[0][0] <= 0:
                    deferred.pop(0)[1]()

            def run_pipeline(items):
                pend = []
                for p1, p2 in items:
                    ctx = p1()
                    pend.append((p2, ctx))
                    if len(pend) > LA:
                        f_, c_ = pend.pop(0)
                        f_(c_)
                        tick()
                for f_, c_ in pend:
                    f_(c_)
                    tick()
                while deferred:
                    deferred.pop(0)[1]()

            swa_items = []
            swa_po = {}
            swa_octr = [0]
            for c in range(4):
                for half in range(2):
                    jjs = [jj for jj in range(5) if not (t == 0 and jj == 0)]
                    for jj in jjs:
                        def p1(c=c, half=half, jj=jj, jjs=jjs):
                            r0 = half * 64
                            if jj == jjs[0]:
                                bk = 6 + (swa_octr[0] % 2)
                                swa_octr[0] += 1
                                swa_po[(c, half)] = (PS[bk], b_ps[bk])
                            qb0 = max(jj - 1, 0); qb1 = min(jj, 3)
                            ncol = (qb1 - qb0 + 1) * 128
                            ps_, bps_ = gps()
                            if jj == 0:
                                mk = maskbias[:, 128:256]
                            elif jj == 4:
                                mk = maskbias[:, 0:128]
                            else:
                                mk = maskbias[:, 0:256]
                            mm(ps_[:, 0:ncol], KAb[r0:r0 + 64, jj * 128:(jj + 1) * 128], QA[r0:r0 + 64, c, qb0 * 128:qb0 * 128 + ncol], True, False, [b_KA, b_QA[c]], bps_)
                            mm(ps_[:, 0:ncol], identb[:, :], mk, False, True, [b_id, b_mask], bps_)
                            pt_, bpt_ = gpt()
                            S.op("act", lambda e: e.activation(out=pt_[:, 0:ncol], in_=ps_[:, 0:ncol], func=AF.Exp, scale=SC_A), reads=[bps_], writes=[bpt_])
                            return (pt_, bpt_, qb0, qb1)

                        def p2(ctx, c=c, half=half, jj=jj, jjs=jjs):
                            pt_, bpt_, qb0, qb1 = ctx
                            r0 = half * 64
                            h = c + 4 * half
                            voff = 0 if half == 0 else 64
                            po, bpo = swa_po[(c, half)]
                            for qb in range(qb0, qb1 + 1):
                                first = (jj == qb) if not (t == 0 and qb == 0) else (jj == 1)
                                last = (jj == qb + 1)
                                lc = (qb - qb0) * 128
                                mm(po[:, qb * 128:(qb + 1) * 128], VAb[:, jj, voff:voff + 128], pt_[:, lc:lc + 128], first, last, [b_VA, bpt_], bpo)
                            if jj == jjs[-1]:
                                def fin(po=po, bpo=bpo, r0=r0, h=h, c=c):
                                    rO = slice(r0, r0 + 64); rZ = slice(64 - r0, 128 - r0)
                                    tz, btz = gtmp(); rz, brz = gtmp()
                                    if t <= 2:
                                        S.op("act", lambda e: e.activation(out=tz[rZ, 0:TT], in_=po[rZ, :], func=AF.Ln, bias=esink[rZ, h:h + 1]), reads=[bpo, b_esink], writes=[btz])
                                        S.op("act", lambda e: e.activation(out=rz[rO, 0:TT], in_=tz[rZ, 0:TT], func=AF.Exp, scale=-1.0), reads=[btz], writes=[brz])
                                    else:
                                        S.op("dve", lambda e: e.tensor_scalar(out=tz[rZ, 0:TT], in0=po[rZ, :], scalar1=esink[rZ, h:h + 1], scalar2=None, op0=ALU.add), reads=[bpo, b_esink], writes=[btz])
                                        S.op("dve", lambda e: e.reciprocal(out=rz[rO, 0:TT], in_=tz[rZ, 0:TT]), reads=[btz], writes=[brz])
                                    S.op("dve", lambda e: e.tensor_tensor(out=ya[rO, c, :], in0=po[rO, :], in1=rz[rO, 0:TT], op=ALU.mult), reads=[bpo, brz], writes=[b_ya])
                                deferred.append([3, fin])
                        swa_items.append((p1, p2))

            S.cur = "t%d:D2" % t
            ring_banks[0] = (0, 1, 2, 3)
            nkc = 4 * (t + 1)
            mla_po = {}
            mla_items = []

            def emit_q(h):
                sl = h % 2
                pq1, bpq1 = gps()
                for k2 in range(2):
                    mm(pq1[0:96, :], wuq[0][:, k2, h * 96:(h + 1) * 96], cqb[:, k2, :], k2 == 0, k2 == 1, [bruq0, b_cqb], bpq1)
                pq2, bpq2 = gps()
                for k2 in range(2):
                    mm(pq2[0:96, :], wuq[1][:, k2, h * 96:(h + 1) * 96], cqb[:, k2, :], k2 == 0, k2 == 1, [bruq1, b_cqb], bpq2)
                S.op("dve", lambda e: e.tensor_tensor(out=QB[0:64, sl, :], in0=pq1[0:64, :], in1=rstdq[0:64, :], op=ALU.mult), reads=[bpq1, b_rq], writes=[b_QB[sl]])
                rope_evac(pq1, bpq1, pq2, bpq2, cosB, sinB, b_tabB, QB[64:96, sl, :], b_QB[sl], 64, 96)

            for h in range(8):
                for kc in range(nkc):
                    def p1(h=h, kc=kc):
                        sl = h % 2
                        if kc == 0 and h == 0:
                            emit_q(0)
                        if kc == 0:
                            mla_po[h] = ops_()
                        if kc == 1 and h + 1 < 8:
                            emit_q(h + 1)
                        jd = kc - 4 * t
                        q0 = 0 if jd <= 0 else jd * 128
                        ncol = TT - q0
                        ps_, bps_ = gps()
                        if jd >= 0:
                            kk = kst[0:96, h, jd * 128:(jd + 1) * 128]
                            mm(ps_[:, 0:128], kk, QB[0:96, sl, q0:q0 + 128], True, False, [b_kst, b_QB[sl]], bps_)
                            mm(ps_[:, 0:128], identb[:, :], maskbias[:, 0:128], False, True, [b_id, b_mask], bps_)
                            if ncol > 128:
                                mm(ps_[:, 128:ncol], kk, QB[0:96, sl, q0 + 128:TT], True, True, [b_kst, b_QB[sl]], bps_)
                        else:
                            kk = kbuf[sl][0:96, kc * 128:(kc + 1) * 128]
                            mm(ps_[:, 0:ncol], kk, QB[0:96, sl, q0:TT], True, True, [b_kbuf[sl], b_QB[sl]], bps_)
                        pt_, bpt_ = gpt()
                        S.op("act", lambda e: e.activation(out=pt_[:, 0:ncol], in_=ps_[:, 0:ncol], func=AF.Exp, scale=SC_B), reads=[bps_], writes=[bpt_])
                        return (pt_, bpt_, q0, ncol)

                    def p2(ctx, h=h, kc=kc):
                        pt_, bpt_, q0, ncol = ctx
                        sl = h % 2
                        po, bpo = mla_po[h]
                        jd2 = kc - 4 * t
                        if jd2 >= 0:
                            mm(po[:, q0:TT], vst[:, jd2, h, :], pt_[:, 0:ncol], kc == 0, kc == nkc - 1, [b_vst_e[jd2], b_vst_o[jd2], b_vones, bpt_], bpo)
                        else:
                            mm(po[:, q0:TT], vbuf[sl][:, kc, :], pt_[:, 0:ncol], kc == 0, kc == nkc - 1, [b_vbuf[sl], bpt_], bpo)
                        if kc == nkc - 1:
                            def fin(po=po, bpo=bpo, h=h):
                                r0 = 0 if h % 2 == 0 else 64
                                rO = slice(r0, r0 + 64); rZ = slice(64 - r0, 128 - r0)
                                tz, btz = gtmp(); rz, brz = gtmp()
                                if t <= 1:
                                    S.op("act", lambda e: e.activation(out=tz[rZ, 0:TT], in_=po[rZ, :], func=AF.Ln), reads=[bpo], writes=[btz])
                                    S.op("act", lambda e: e.activation(out=rz[rO, 0:TT], in_=tz[rZ, 0:TT], func=AF.Exp, scale=-1.0), reads=[btz], writes=[brz])
                                else:
                                    S.op("dve", lambda e: e.reciprocal(out=rz[rO, 0:TT], in_=po[rZ, :]), reads=[bpo], writes=[brz])
                                S.op("dve", lambda e: e.tensor_tensor(out=yb[rO, h // 2, :], in0=po[rO, :], in1=rz[rO, 0:TT], op=ALU.mult), reads=[bpo, brz], writes=[b_yb])
                            deferred.append([3, fin])
                            if h + 2 < 8:
                                load_kv(h + 2)
                    mla_items.append((p1, p2))
            merged = []
            n_m, n_s = len(mla_items), len(swa_items)
            si = 0
            for mi, it in enumerate(mla_items):
                merged.append(it)
                while si < n_s and (si + 1) * n_m <= (mi + 1) * n_s:
                    merged.append(swa_items[si]); si += 1
            merged.extend(swa_items[si:])
            run_pipeline(merged)
            S.op("pool", lambda e: e.tensor_copy(out=KAb[:, 0:128], in_=KAb[:, 512:640]), reads=[b_KA], writes=[b_KA])
            S.op("pool", lambda e: e.tensor_copy(out=VAb[:, 0, :], in_=VAb[:, 4, :]), reads=[b_VA], writes=[b_VA])

            ring_banks[0] = (0, 1, 2, 3)
            if t == 0:
                dump("ya", ya[:], b_ya, [128, 4, TT], BF16); dump("yb", yb[:], b_yb, [128, 4, TT], BF16)
            S.cur = "t%d:E" % t
            for n in range(8):
                rg, brg = load_ring(WG_s[n].rearrange("p k n -> p (k n)"), "WG%d" % n, 2048)
                wg = rg[:, 0:2048].rearrange("p (k n) -> p k n", n=256)
                rab, brab = load_ring(WAB_s[n].rearrange("p k n -> p (k n)"), "WAB%d" % n, 1024)
                wab = rab[:, 0:1024].rearrange("p (k n) -> p k n", n=128)
                gts = []
                for gi_ in range(2):
                    pg, bpg = gps()
                    for kc in range(8):
                        mm(pg[:, :], wg[:, kc, gi_ * 128:(gi_ + 1) * 128], hT[:, kc, :], kc == 0, kc == 7, [brg, b_hT], bpg)
                    gt, bgt = gtmp()
                    S.op("act", lambda e, pg=pg, gt=gt, gi_=gi_, n=n: e.activation(out=gt[:, 0:TT], in_=pg[:, :], func=AF.Sigmoid, bias=BG[:, gi_ * 8 + n:gi_ * 8 + n + 1]), reads=[bpg, b_C], writes=[bgt])
                    gts.append((gt, bgt))
                for gi_, (yy, byy) in enumerate(((ya, b_ya), (yb, b_yb))):
                    pa, bpa = gps()
                    for kc in range(4):
                        mm(pa[:, :], wab[:, gi_ * 4 + kc, :], yy[:, kc, :], kc == 0, kc == 3, [brab, byy], bpa)
                    gt, bgt = gts[gi_]
                    S.op("dve", lambda e, pa=pa, gt=gt: e.tensor_tensor(out=gt[:, 0:TT], in0=gt[:, 0:TT], in1=pa[:, :], op=ALU.mult), reads=[bpa, bgt], writes=[bgt])
                S.op("pool", lambda e, n=n, g0=gts[0][0], g1=gts[1][0]: e.tensor_tensor(out=mixed[:, n, :], in0=g0[:, 0:TT], in1=g1[:, 0:TT], op=ALU.add), reads=[gts[0][1], gts[1][1]], writes=[b_mixed, b_kst])

            if t + 1 < NT:
                load_kv_past(t + 1, 0)
                load_kv_past(t + 1, 1)
            S.cur = "t%d:F" % t
            for nh in range(2):
                load_big(nh, WOUT_s[nh].rearrange("p k n -> p (k n)"), "WOUT%d" % nh, 4096)
            wo = [big[nh][:, 0:4096].rearrange("p (k n) -> p k n", n=512) for nh in range(2)]
            for jp in range(2):
                items = []
                for j in (2 * jp, 2 * jp + 1):
                    pp = []
                    for nh in range(2):
                        ps_, bps_ = gps()
                        for kc in range(8):
                            mm(ps_[:, :], mixed[:, kc, j * 128:(j + 1) * 128], wo[nh][:, kc, :], kc == 0, kc == 7, [b_mixed, b_big[nh]], bps_)
                        pp.append((ps_, bps_))
                    items.append((j, pp[0][0], pp[0][1], pp[1][0], pp[1][1]))
                post_norm_residual_multi(items, 0)

            if t == 0:
                dump("mixed", mixed[:], b_mixed, [128, 8, TT], BF16); dump("x1", xt[:], b_xtj[3], [128, 4, D])
            S.cur = "t%d:G" % t
            norm_T("mlp")
            for u_ in range(2):
                load_big(u_, WDN_s[u_].rearrange("p k n -> p (k n)"), "WDN%d" % u_, 5632)
            ring_banks[0] = (0, 1, 2, 3, 4, 5, 6, 7)
            pend_fin = []
            for f in range(22):
                ru, bru = load_ring(WUP_s[f].rearrange("p a k n -> p (a k n)"), "WUP%d" % f, 2048)
                ys = []
                for gv in range(2):
                    wv = ru[:, gv * 1024:(gv + 1) * 1024].rearrange("p (k n) -> p k n", n=128)
                    pu, bpu = proj_fm(wv, bru, hT, b_hT)
                    idx = gv * 22 + f
                    y, by = gtmp()
                    z, bz = gtmp()
                    bh = b_halo_i[idx]
                    S.op("act", lambda e, y=y, pu=pu, idx=idx: e.activation(out=y[:, 0:TT], in_=pu[:, :], func=AF.Identity, scale=CW[2][:, idx:idx + 1], bias=CBIAS[:, idx:idx + 1]), reads=[bpu, b_C], writes=[by])
                    S.op("act", lambda e, z=z, pu=pu, idx=idx: e.activation(out=z[:, 2:TT], in_=pu[:, 0:TT - 2], func=AF.Identity, scale=CW[0][:, idx:idx + 1]), reads=[bpu, b_C], writes=[bz])
                    S.op("act", lambda e, z=z, idx=idx: e.activation(out=z[:, 0:2], in_=halo[:, idx, 0:2], func=AF.Identity, scale=CW[0][:, idx:idx + 1]), reads=[bh, b_C], writes=[bz])
                    S.op("dve", lambda e, y=y, idx=idx: e.scalar_tensor_tensor(out=y[:, 0:1], in0=halo[:, idx, 1:2], scalar=CW[1][:, idx:idx + 1], in1=y[:, 0:1], op0=ALU.mult, op1=ALU.add), reads=[bh, by, bz, b_C], writes=[by])
                    S.op("dve", lambda e, pu=pu, idx=idx: e.tensor_copy(out=halo[:, idx, :], in_=pu[:, TT - 2:TT]), reads=[bpu, bz], writes=[bh])
                    S.op("dve", lambda e, y=y, pu=pu, idx=idx: e.scalar_tensor_tensor(out=y[:, 1:TT], in0=pu[:, 0:TT - 1], scalar=CW[1][:, idx:idx + 1], in1=y[:, 1:TT], op0=ALU.mult, op1=ALU.add), reads=[bpu, by, bz, b_C], writes=[by])
                    en = "dve" if gv == 0 else "pool"
                    S.op(en, lambda e, y=y, z=z: e.tensor_tensor(out=y[:, 0:TT], in0=y[:, 0:TT], in1=z[:, 0:TT], op=ALU.add), reads=[by, bz], writes=[by])
                    ys.append((y, by))
                def fin(f=f, ys=ys):
                    (yg, byg), (yv, byv) = ys
                    S.op("act", lambda e: e.activation(out=yg[:, 0:TT], in_=yg[:, 0:TT], func=AF.Gelu_apprx_tanh), reads=[byg], writes=[byg])
                    S.op("pool", lambda e: e.tensor_tensor(out=act[:, f, :], in0=yg[:, 0:TT], in1=yv[:, 0:TT], op=ALU.mult), reads=[byg, byv], writes=W_ACT)
                if pend_fin:
                    pend_fin.pop(0)()
                pend_fin.append(fin)
            while pend_fin:
                pend_fin.pop(0)()
            ring_banks[0] = (0, 1, 2, 3)
            if t + 1 < NT:
                S.cur = "t%d:A" % (t + 1)
                prep_tables(t + 1)
            S.cur = "t%d:G2" % t
            for nh in range(2):
                for fh in range(2):
                    u = nh * 2 + fh
                    if u >= 2:
                        load_big(u % 2, WDN_s[u].rearrange("p k n -> p (k n)"), "WDN%d" % u, 5632)
                    wd = big[u % 2][:, 0:5632].rearrange("p (k n) -> p k n", n=512)
                    for j in range(4):
                        bk = nh * 4 + j
                        for fi in range(11):
                            f = fh * 11 + fi
                            mm(PS[bk][:, :], act[:, f, j * 128:(j + 1) * 128], wd[:, fi, :], f == 0, f == 21, R_ACT + [b_big[u % 2]], b_ps[bk])
            post_norm_residual_multi([(j, PS[j], b_ps[j], PS[4 + j], b_ps[4 + j]) for j in range(4)], 1)

            if t == 0:
                dump("act", act[:], b_act, [128, 22, TT], BF16); dump("x2", xt[:], b_xtj[3], [128, 4, D])
            S.cur = "t%d:H" % t
            norm_T("ple")
            pf_, bpf_ = gtmp(); pf2, bpf2 = gtmp()
            S.dma("sp", lambda e, tok0=tok0, pf_=pf_, pf2=pf2: e.dma_start(out=pf_[:, 0:512].rearrange("p (j d) -> p j d", d=256), in_=p_d[tok0:tok0 + 256, :].rearrange("(j p) d -> p j d", p=128)), bpf_, writes=[bpf_])
            S.dma("sp", lambda e, tok0=tok0, pf2=pf2: e.dma_start(out=pf2[:, 0:512].rearrange("p (j d) -> p j d", d=256), in_=p_d[tok0 + 256:tok0 + 512, :].rearrange("(j p) d -> p j d", p=128)), bpf2, writes=[bpf2])
            S.op("pool", lambda e, pf_=pf_: e.tensor_copy(out=hn[:, 0:512], in_=pf_[:, 0:512]), reads=[bpf_], writes=[b_hn])
            S.op("pool", lambda e, pf2=pf2: e.tensor_copy(out=hn[:, 512:1024], in_=pf2[:, 0:512]), reads=[bpf2], writes=[b_hn])
            pb, pf, bp = tps()
            for j in range(4):
                for c in range(2):
                    S.op("pe", lambda e, j=j, c=c, pb=pb: e.transpose(out=pb[:, c * 512 + j * 128:c * 512 + (j + 1) * 128], in_=hn[:, j * 256 + c * 128:j * 256 + (c + 1) * 128], identity=identb[:]), reads=[b_hn, b_id], writes=[bp])
            S.op("act", lambda e, pb=pb: e.copy(out=ptT[:, :, :], in_=pb[:, 0:1024].rearrange("p (c t) -> p c t", t=512)), reads=[bp], writes=[b_ptT])
            rp, brp = load_ring(WPLE_s.rearrange("p k n -> p (k n)"), "WPLE", 2048)
            wple = rp[:, 0:2048].rearrange("p (k n) -> p k n", n=1024)
            for nh in range(2):
                load_big(nh, WPG_s[nh].rearrange("p k n -> p (k n)"), "WPG%d" % nh, 4096)
            wpg = [big[nh][:, 0:4096].rearrange("p (k n) -> p k n", n=512) for nh in range(2)]
            for j in range(4):
                for nh in range(2):
                    pg, bpg = gps()
                    for kc in range(8):
                        mm(pg[:, :], hT[:, kc, j * 128:(j + 1) * 128], wpg[nh][:, kc, :], kc == 0, kc == 7, [b_hT, b_big[nh]], bpg)
                    pe_, bpe_ = gps()
                    for kc in range(2):
                        mm(pe_[:, :], ptT[:, kc, j * 128:(j + 1) * 128], wple[:, kc, nh * 512:(nh + 1) * 512], kc == 0, kc == 1, [b_ptT, brp], bpe_)
                    sg, bsg = gtmp()
                    S.op("act", lambda e, pg=pg, sg=sg: e.activation(out=sg[:, 0:TT], in_=pg[:, :], func=AF.Sigmoid), reads=[bpg], writes=[bsg])
                    S.op("dve", lambda e, pe_=pe_, sg=sg: e.tensor_tensor(out=sg[:, 0:TT], in0=sg[:, 0:TT], in1=pe_[:, :], op=ALU.mult), reads=[bpe_, bsg], writes=[bsg])
                    S.op("pool", lambda e, j=j, nh=nh, sg=sg: e.tensor_tensor(out=xt[:, j, nh * 512:(nh + 1) * 512], in0=xt[:, j, nh * 512:(nh + 1) * 512], in1=sg[:, 0:TT], op=ALU.add), reads=[bsg, b_xtj[j]], writes=[b_xtj[j]])
                S.dma("sp", lambda e, tok0=tok0, j=j: e.dma_start(out=out_d[tok0 + j * 128:tok0 + (j + 1) * 128, :], in_=xt[:, j, :]), b_xtj[j], reads=[b_xtj[j]])
                if t + 1 < NT:
                    load_x(t + 1, j)

        for t_ in range(NT):
            do_tile(t_)

        S.emit()
        print("sched stats", S.stats, flush=True)
        import os as _os
        if _os.environ.get("KLABELS"):
            import json as _json
            _json.dump(S.labels, open(_os.environ["KLABELS"], "w"))
    return nc


_CACHE = {}
_DBG = False
_LAST = {}


def _consts():
    c = np.zeros((128, 4), np.float32)
    p = np.arange(128)
    invA = (10000.0 ** (-(np.arange(0, 64, 2, dtype=np.float32) / 64))).astype(np.float32)
    invB = (10000.0 ** (-(np.arange(0, 32, 2, dtype=np.float32) / 32))).astype(np.float32)
    c[:, 0] = invA[p % 32]
    c[:, 1] = invB[p % 16]
    return c


def kernel(**inputs):
    x = np.asarray(inputs["x"])
    Bn, SEQ, _ = x.shape
    key = SEQ
    if key not in _CACHE:
        _CACHE[key] = build_program(SEQ, dbg=_DBG)
    nc = _CACHE[key]
    shared = {"cst": _consts()}
    for k in ("w_in", "w_uq", "w_ukv", "w_branch_a", "w_branch_b", "w_out", "w_up", "w_down", "w_ple_gate", "w_ple", "conv_w"):
        shared[k] = np.ascontiguousarray(np.asarray(inputs[k])[0], dtype=np.float32)
    for k in ("conv_b", "b_gate", "sinks", "attn_pre_norm", "attn_post_norm", "mlp_pre_norm", "mlp_post_norm", "ple_norm", "q_a_norm", "kv_a_norm"):
        shared[k] = np.ascontiguousarray(np.asarray(inputs[k]), dtype=np.float32).reshape(1, -1)
    p = np.asarray(inputs["p"])[0]
    pos = np.asarray(inputs["positions"]).astype(np.int32)
    in_maps = []
    for b in range(Bn):
        m = dict(shared)
        m["x"] = np.ascontiguousarray(x[b], dtype=np.float32)
        m["p"] = np.ascontiguousarray(p[b], dtype=np.float32)
        m["pos"] = np.ascontiguousarray(pos[b].reshape(1, SEQ))
        in_maps.append(m)
    res = run_bass_kernel_spmd(nc, in_maps, core_ids=list(range(Bn)))
    if _DBG:
        _LAST.update(res.results[0])
    out = np.stack([np.asarray(r["out"], dtype=np.float32) for r in res.results], axis=0)
    return out
```
